# Optimizing a Trainium2 kernel written in Bass

```python
import math, functools
import jax, jax.numpy as jnp
from jax import lax
import numpy as np

D_MODEL = 1024
BATCH = 8
SEQ = 2048
DEPTH = 1
DEC_BATCH = 128
DEC_SEQ = 1
PAST_LEN = 2048
PAGE_SIZE = 128

PLE_DIM = 256
ATT_WIDTH = D_MODEL // 2
REC_WIDTH = D_MODEL - ATT_WIDTH
ATT_HEADS = 4
ATT_HD = ATT_WIDTH // (2 * ATT_HEADS)
REC_HEADS = 4
REC_DK = REC_WIDTH // REC_HEADS
REC_DV = REC_WIDTH // REC_HEADS
IN_WIDTH = 3 * ATT_WIDTH + 4 * REC_WIDTH
D_FF = 4 * D_MODEL
CHUNK = 64
Q_BLOCK = 128
EPS = 1e-6

kernel_name = "hymba_diffattn_hgrn2_step"


def rmsnorm(x, g):
    xf = x.astype(jnp.float32)
    y = xf * lax.rsqrt(jnp.mean(xf * xf, axis=-1, keepdims=True) + EPS)
    return (y * g.astype(jnp.float32)).astype(x.dtype)


def lambda_init_fn(li):
    return 0.8 - 0.6 * math.exp(-0.3 * li)


def diff_attend(q, k, v, q_pos, k_pos, lam):
    s = jnp.einsum('bqhmd,bkhmd->bhmqk', q, k).astype(jnp.float32) * (ATT_HD ** -0.5)
    mask = k_pos[None, :] <= q_pos[:, None]
    s = jnp.where(mask, s, -jnp.inf)
    a = jax.nn.softmax(s, axis=-1)
    w = a[:, :, 0] - lam * a[:, :, 1]
    return jnp.einsum('bhqk,bkhe->bqhe', w.astype(v.dtype), v)


def prompt_attn(q, k, v, lam):
    B, S = q.shape[:2]
    nb = S // Q_BLOCK
    qb = q.reshape(B, nb, Q_BLOCK, ATT_HEADS, 2, ATT_HD).transpose(1, 0, 2, 3, 4, 5)
    pb = jnp.arange(S).reshape(nb, Q_BLOCK)
    kk = k.reshape(B, S, ATT_HEADS, 2, ATT_HD)
    kpos = jnp.arange(S)
    ob = lax.map(lambda a: diff_attend(a[0], kk, v, a[1], kpos, lam), (qb, pb))
    return ob.transpose(1, 0, 2, 3, 4).reshape(B, S, ATT_HEADS, 2 * ATT_HD)


def sample_attn(q, k, v, lam, ck, cv, page_table):
    B, T = q.shape[:2]
    past_k = ck[page_table].reshape(B, -1, ATT_HEADS, 2 * ATT_HD)
    past_v = cv[page_table].reshape(B, -1, ATT_HEADS, 2 * ATT_HD)
    past = past_k.shape[1]
    kk = jnp.concatenate([past_k.astype(k.dtype), k], axis=1).reshape(B, past + T, ATT_HEADS, 2, ATT_HD)
    vv = jnp.concatenate([past_v.astype(v.dtype), v], axis=1)
    q_pos = past + jnp.arange(T)
    k_pos = jnp.arange(past + T)
    q = q.reshape(B, T, ATT_HEADS, 2, ATT_HD)
    return diff_attend(q, kk, vv, q_pos, k_pos, lam)


def hgrn_scan(q, k, v, logf, s0):
    B, T, H, _ = q.shape
    c = min(CHUNK, T)
    n = -(-T // c)
    pad = n * c - T
    if pad:
        padw = ((0, 0), (0, pad), (0, 0), (0, 0))
        q = jnp.pad(q, padw); k = jnp.pad(k, padw); v = jnp.pad(v, padw); logf = jnp.pad(logf, padw)

    def to_chunks(a):
        return a.reshape(B, n, c, H, a.shape[-1]).transpose(1, 0, 3, 2, 4)

    causal = jnp.tril(jnp.ones((c, c), dtype=bool))[:, :, None]

    def step(S, inp):
        qc, kc, vc, lc = inp
        b = jnp.cumsum(lc, axis=2)
        o_inter = jnp.einsum('bhtd,bhde->bhte', qc * jnp.exp(b), S)
        diff = b[:, :, :, None, :] - b[:, :, None, :, :]
        decay = jnp.exp(jnp.where(causal, diff, -jnp.inf))
        att = jnp.einsum('bhtd,bhsd,bhtsd->bhts', qc, kc, decay)
        o = o_inter + jnp.einsum('bhts,bhse->bhte', att, vc)
        b_last = b[:, :, -1:, :]
        S_new = jnp.exp(b_last[:, :, 0, :])[..., None] * S + jnp.einsum(
            'bhsd,bhse->bhde', kc * jnp.exp(b_last - b), vc)
        return S_new, o

    s_fin, o = lax.scan(step, s0, (to_chunks(q), to_chunks(k), to_chunks(v), to_chunks(logf)))
    o = o.transpose(1, 0, 3, 2, 4).reshape(B, n * c, H, v.shape[-1])[:, :T]
    return o, s_fin


def hgrn_mixer(rq, rf, ri, rg, lb, g_rec, s0):
    B, T, _ = rq.shape
    f32 = jnp.float32
    shp = (B, T, REC_HEADS, REC_DK)
    q = jax.nn.silu(rq.astype(f32)).reshape(shp)
    f = lb + (1.0 - lb) * jax.nn.sigmoid(rf.astype(f32))
    logf = jnp.log(f).reshape(shp)
    k = (1.0 - f).reshape(shp)
    v = ri.astype(f32).reshape(B, T, REC_HEADS, REC_DV)
    o, s_fin = hgrn_scan(q, k, v, logf, s0.astype(f32))
    o = rmsnorm(o.reshape(B, T, REC_WIDTH), g_rec) * jax.nn.silu(rg.astype(f32))
    return o.astype(rq.dtype), s_fin


def block(x, p_i, li, attend, s0, w_in, lambda_q1, lambda_k1, lambda_q2, lambda_k2,
          g_subln, hgrn_lb, g_rec, w_out, g_mix, g_ffn, w_up, w_down, w_ple_gate, w_ple_proj):
    B, T, _ = x.shape
    h = rmsnorm(x, g_mix[li])
    z = h @ w_in[li]
    cuts = [ATT_WIDTH, 2 * ATT_WIDTH, 3 * ATT_WIDTH, 3 * ATT_WIDTH + REC_WIDTH,
            3 * ATT_WIDTH + 2 * REC_WIDTH, 3 * ATT_WIDTH + 3 * REC_WIDTH]
    aq, ak, av, rq, rf, ri, rg = jnp.split(z, cuts, axis=-1)
    lam_init = lambda_init_fn(li)
    f32 = jnp.float32
    lam = (jnp.exp(jnp.sum(lambda_q1[li].astype(f32) * lambda_k1[li].astype(f32)))
           - jnp.exp(jnp.sum(lambda_q2[li].astype(f32) * lambda_k2[li].astype(f32))) + lam_init)
    q = aq.reshape(B, T, ATT_HEADS, 2, ATT_HD)
    k = ak.reshape(B, T, ATT_HEADS, 2 * ATT_HD)
    v = av.reshape(B, T, ATT_HEADS, 2 * ATT_HD)
    att = attend(q, k, v, lam)
    att = (rmsnorm(att, g_subln[li]) * (1.0 - lam_init)).reshape(B, T, ATT_WIDTH)
    lb = jnp.cumsum(jax.nn.softmax(hgrn_lb.astype(f32), axis=0), axis=0)[li]
    rec, s_new = hgrn_mixer(rq, rf, ri, rg, lb, g_rec[li], s0)
    mix = jnp.concatenate([att.astype(x.dtype), rec.astype(x.dtype)], axis=-1) @ w_out[li]
    x = x + mix
    hf = rmsnorm(x, g_ffn[li]) @ w_up[li]
    x = x + jnp.square(jax.nn.relu(hf)) @ w_down[li]
    x = x + jax.nn.sigmoid(x @ w_ple_gate[li]) * (p_i @ w_ple_proj[li])
    return x, k, v, s_new


def setup_inputs(seed: int = 0) -> dict:
    key = jax.random.key(seed)
    ks = jax.random.split(key, 26)
    f32 = jnp.float32
    n_pages = PAST_LEN // PAGE_SIZE
    n_pool = (DEC_BATCH * n_pages * 5) // 4

    def nrm(k, shape, scale):
        return jax.random.normal(k, shape, f32) * scale

    page_table = jax.random.permutation(ks[5], n_pool)[:DEC_BATCH * n_pages]
    page_table = page_table.reshape(DEC_BATCH, n_pages).astype(jnp.int32)
    return {
        'x_prompt': nrm(ks[0], (BATCH, SEQ, D_MODEL), 1.0),
        'x_sample': nrm(ks[1], (DEC_BATCH, DEC_SEQ, D_MODEL), 1.0),
        'cache_k': nrm(ks[2], (DEPTH, n_pool, PAGE_SIZE, ATT_HEADS, 2 * ATT_HD), 1.0),
        'cache_v': nrm(ks[3], (DEPTH, n_pool, PAGE_SIZE, ATT_HEADS, 2 * ATT_HD), 1.0),
        'state_hgrn': nrm(ks[4], (DEPTH, DEC_BATCH, REC_HEADS, REC_DK, REC_DV), 0.2),
        'page_table': page_table,
        'p_prompt': nrm(ks[6], (DEPTH, BATCH, SEQ, PLE_DIM), 1.0),
        'p_sample': nrm(ks[7], (DEPTH, DEC_BATCH, DEC_SEQ, PLE_DIM), 1.0),
        'w_in': nrm(ks[8], (DEPTH, D_MODEL, IN_WIDTH), D_MODEL ** -0.5),
        'lambda_q1': nrm(ks[9], (DEPTH, ATT_HD), 0.1),
        'lambda_k1': nrm(ks[10], (DEPTH, ATT_HD), 0.1),
        'lambda_q2': nrm(ks[11], (DEPTH, ATT_HD), 0.1),
        'lambda_k2': nrm(ks[12], (DEPTH, ATT_HD), 0.1),
        'g_subln': 1.0 + nrm(ks[13], (DEPTH, 2 * ATT_HD), 0.02),
        'hgrn_lb': nrm(ks[14], (DEPTH + 1, REC_WIDTH), 0.1),
        'g_rec': 1.0 + nrm(ks[15], (DEPTH, REC_WIDTH), 0.02),
        'w_out': nrm(ks[16], (DEPTH, D_MODEL, D_MODEL), D_MODEL ** -0.5),
        'g_mix': 1.0 + nrm(ks[17], (DEPTH, D_MODEL), 0.02),
        'g_ffn': 1.0 + nrm(ks[18], (DEPTH, D_MODEL), 0.02),
        'w_up': nrm(ks[19], (DEPTH, D_MODEL, D_FF), D_MODEL ** -0.5),
        'w_down': nrm(ks[20], (DEPTH, D_FF, D_MODEL), D_FF ** -0.5),
        'w_ple_gate': nrm(ks[21], (DEPTH, D_MODEL, D_MODEL), D_MODEL ** -0.5),
        'w_ple_proj': nrm(ks[22], (DEPTH, PLE_DIM, D_MODEL), PLE_DIM ** -0.5),
        'g_final': 1.0 + nrm(ks[23], (D_MODEL,), 0.02),
    }


def reference(x_prompt, x_sample, cache_k, cache_v, state_hgrn, page_table, p_prompt, p_sample,
              w_in, lambda_q1, lambda_k1, lambda_q2, lambda_k2, g_subln, hgrn_lb, g_rec, w_out,
              g_mix, g_ffn, w_up, w_down, w_ple_gate, w_ple_proj, g_final):
    weights = (w_in, lambda_q1, lambda_k1, lambda_q2, lambda_k2, g_subln, hgrn_lb, g_rec, w_out,
               g_mix, g_ffn, w_up, w_down, w_ple_gate, w_ple_proj)
    hp, hs = x_prompt, x_sample
    kp_l, vp_l, sp_l, ks_l, vs_l, ss_l = [], [], [], [], [], []
    for li in range(DEPTH):
        s0_p = jnp.zeros((hp.shape[0], REC_HEADS, REC_DK, REC_DV), jnp.float32)
        hp, kp, vp, sp = block(hp, p_prompt[li], li, prompt_attn, s0_p, *weights)
        attend_s = functools.partial(sample_attn, ck=cache_k[li], cv=cache_v[li], page_table=page_table)
        hs, ksm, vsm, ssm = block(hs, p_sample[li], li, attend_s, state_hgrn[li], *weights)
        kp_l.append(kp); vp_l.append(vp); sp_l.append(sp)
        ks_l.append(ksm); vs_l.append(vsm); ss_l.append(ssm)
    y_prompt = rmsnorm(hp, g_final)
    y_sample = rmsnorm(hs, g_final)
    k_prompt = jnp.stack(kp_l, 0)
    v_prompt = jnp.stack(vp_l, 0)
    s_prompt = jnp.stack(sp_l, 0)
    k_sample = jnp.stack(ks_l, 0)
    v_sample = jnp.stack(vs_l, 0)
    s_sample = jnp.stack(ss_l, 0)
    return (y_prompt, y_sample, k_prompt, v_prompt, s_prompt, k_sample, v_sample, s_sample)
```

```python
from contextlib import ExitStack

import numpy as np
import concourse.bass as bass
import concourse.mybir as mybir
from concourse.bass_utils import run_bass_kernel_spmd

F32 = mybir.dt.float32
BF16 = mybir.dt.bfloat16
I32 = mybir.dt.int32
AF = mybir.ActivationFunctionType
ALU = mybir.AluOpType
AX = mybir.AxisListType

D = 1024
NC8 = 8
INW = 3584
EPS = 1e-6
COMPUTE = ("pe", "act", "dve", "pool")


class Ins:
    __slots__ = ("eng", "fn", "deps", "inc", "sem", "val", "is_dma")

    def __init__(self, eng, fn, is_dma=False):
        self.eng = eng
        self.fn = fn
        self.deps = []
        self.inc = False
        self.sem = None
        self.val = None
        self.is_dma = is_dma


class Buf:
    __slots__ = ("name", "w", "r")

    def __init__(self, name=""):
        self.name = name
        self.w = []
        self.r = []


class Prog:
    def __init__(self, nc, ndma=None):
        self.nc = nc
        self.q = {e: [] for e in ("pe", "act", "dve", "pool", "sp")}
        self.ndma = ndma or {"sp": 32, "pool": 24, "act": 8}
        self.dma_hist = {k: [] for k in self.ndma}
        self.all_dma = []
        self.pending = {}

    def barrier(self):
        deps = []
        for e, q in self.q.items():
            last = None
            for ins in reversed(q):
                if not ins.is_dma:
                    last = ins
                    break
            if last is not None:
                deps.append(last)
        for qn, hist in self.dma_hist.items():
            deps.extend(hist[-self.ndma[qn]:])
        for e in self.q:
            self.pending[e] = list(deps)

    def _track(self, ins, reads, writes, deps):
        ds = []
        for b in reads:
            ds.extend(b.w)
        for b in writes:
            ds.extend(b.w)
            ds.extend(b.r)
        ds.extend(deps)
        seen = set()
        for d in ds:
            if d is None or id(d) in seen or d is ins:
                continue
            seen.add(id(d))
            if (not d.is_dma) and (not ins.is_dma) and d.eng == "pe" and ins.eng == "pe":
                continue
            ins.deps.append(d)
            d.inc = True
        for b in writes:
            b.w = [ins]
            b.r = []
        for b in reads:
            if b not in writes:
                b.r.append(ins)

    def op(self, eng, fn, reads=(), writes=(), deps=()):
        ins = Ins(eng, fn)
        deps = list(deps) + self.pending.pop(eng, [])
        self._track(ins, reads, writes, deps)
        self.q[eng].append(ins)
        return ins

    def dma(self, queue, fn, reads=(), writes=(), deps=()):
        ins = Ins(queue, fn, is_dma=True)
        ins.inc = True
        hist = self.dma_hist[queue]
        n = self.ndma[queue]
        j = len(hist)
        extra = list(deps) + self.pending.pop(queue, [])
        if j >= n:
            extra.append(hist[j - n])
        self._track(ins, reads, writes, extra)
        ins.sem = (queue, j % n)
        ins.val = 16 * (j // n + 1)
        hist.append(ins)
        self.all_dma.append(ins)
        self.q[queue].append(ins)
        return ins

    def emit(self, stack):
        nc = self.nc
        esem = {e: stack.enter_context(nc.semaphore("s_" + e)) for e in COMPUTE}
        dsem = {}
        for qn, n in self.ndma.items():
            for i in range(n):
                dsem[(qn, i)] = stack.enter_context(nc.semaphore("d_%s%d" % (qn, i)))
        for e in ("pe", "act", "dve", "pool", "sp"):
            c = 0
            for ins in self.q[e]:
                if ins.is_dma:
                    ins.sem = dsem[ins.sem]
                elif ins.inc:
                    c += 1
                    ins.sem = esem[e]
                    ins.val = c
        final = {}
        for d in self.all_dma:
            k = d.sem.num
            if k not in final or final[k][1] < d.val:
                final[k] = (d.sem, d.val)
        block = stack.enter_context(nc.Block())

        def run(engname, eng):
            waited = {}
            for ins in self.q[engname]:
                for d in ins.deps:
                    k = d.sem.num
                    if waited.get(k, 0) >= d.val:
                        continue
                    eng.wait_ge(d.sem, d.val)
                    waited[k] = d.val
                bi = ins.fn(eng)
                if ins.inc:
                    bi.then_inc(ins.sem, 16 if ins.is_dma else 1)
            if engname == "sp":
                for k, (s, v) in final.items():
                    if waited.get(k, 0) < v:
                        eng.wait_ge(s, v)

        block.tensor(lambda e: run("pe", e))
        block.scalar(lambda e: run("act", e))
        block.vector(lambda e: run("dve", e))
        block.gpsimd(lambda e: run("pool", e))
        block.sync(lambda e: run("sp", e))


def build_nc(S=2048, NS=16, dbg=False, NPG=16, NPOOL=2560):
    assert S % 128 == 0
    NB = S // 128
    nc = bass.Bass("TRN2", target_bir_lowering=False)

    def din(name, shape, dt=F32):
        return nc.dram_tensor(name, shape, dt, kind="ExternalInput").ap()

    def dout(name, shape, dt=F32):
        return nc.dram_tensor(name, shape, dt, kind="ExternalOutput").ap()

    x_p = din("x_p", [S, D])
    x_s = din("x_s", [NS, D])
    w_in = din("w_in", [D, INW])
    g_mix = din("g_mix", [D])
    hgrn_lb = din("hgrn_lb", [2, 512])
    g_rec = din("g_rec", [512])
    state_s = din("state_s", [NS * 4 * 128, 128])
    lam_in = din("lam_in", [4, 64])
    g_subln = din("g_subln", [128])
    w_out = din("w_out", [D, D])
    g_ffn = din("g_ffn", [D])
    w_up = din("w_up", [D, 4096])
    w_down = din("w_down", [4096, D])
    w_gate = din("w_gate", [D, D])
    w_proj = din("w_proj", [256, D])
    p_p = din("p_p", [S, 256])
    p_s = din("p_s", [NS, 256])
    g_final = din("g_final", [D])
    cache_k = din("cache_k", [NPOOL * 128, 512])
    cache_v = din("cache_v", [NPOOL * 128, 512])
    page_tab = din("page_tab", [NS * NPG], I32)

    o_yp = dout("o_yp", [S, D])
    o_ys = dout("o_ys", [NS, D])
    o_kp = dout("o_kp", [S, 512])
    o_vp = dout("o_vp", [S, 512])
    o_sp = dout("o_sp", [4 * 128, 128])
    o_ks = dout("o_ks", [NS, 512])
    o_vs = dout("o_vs", [NS, 512])
    o_ss = dout("o_ss", [NS * 4 * 128, 128])
    if dbg:
        d_rec = dout("d_rec", [S + NS, 512])
        d_att = dout("d_att", [S + NS, 512])

    SB_LO, SB_HI = 16512, 229344
    cur = [SB_LO]
    DTB = {F32: 4, BF16: 2, I32: 4}

    with ExitStack() as st:
        def sb(name, shape, dt):
            n = DTB[dt]
            for d_ in shape[1:]:
                n *= d_
            off = (cur[0] + 31) // 32 * 32
            if off + n > SB_HI:
                raise AssertionError("SBUF overflow placing %s: need %d at %d (limit %d)" % (name, n, off, SB_HI))
            cur[0] = off + n
            return nc.alloc_sbuf_tensor_at(name, list(shape), dt, offset=off)

        P = Prog(nc)

        def ACT(out, in_, func, R, W, **kw):
            return P.op("act", lambda e: e.activation(out=out, in_=in_, func=func, **kw), reads=R, writes=W)

        def TT(out, in0, in1, op, R, W, eng="dve"):
            return P.op(eng, lambda e: e.tensor_tensor(out=out, in0=in0, in1=in1, op=op), reads=R, writes=W)

        def TS(out, in0, s1, s2, op0, op1, R, W, eng="dve"):
            if op1 is None:
                return P.op(eng, lambda e: e.tensor_scalar(out=out, in0=in0, scalar1=s1, scalar2=None, op0=op0),
                            reads=R, writes=W)
            return P.op(eng, lambda e: e.tensor_scalar(out=out, in0=in0, scalar1=s1, scalar2=s2, op0=op0, op1=op1),
                        reads=R, writes=W)

        def STT(out, in0, scalar, in1, op0, op1, R, W, eng="dve"):
            return P.op(eng, lambda e: e.scalar_tensor_tensor(out=out, in0=in0, scalar=scalar, in1=in1,
                                                              op0=op0, op1=op1), reads=R, writes=W)

        def CP(out, in_, R, W, eng="dve"):
            if eng == "act":
                return P.op("act", lambda e: e.activation(out=out, in_=in_, func=AF.Copy), reads=R, writes=W)
            return P.op(eng, lambda e: e.tensor_copy(out=out, in_=in_), reads=R, writes=W)

        def RECIP(out, in_, R, W):
            return P.op("dve", lambda e: e.reciprocal(out=out, in_=in_), reads=R, writes=W)

        def MSET(ap, v, W, eng="pool"):
            return P.op(eng, lambda e: e.memset(ap, v), writes=W)

        def ASEL(ap, pattern, cmp, cm, W):
            return P.op("pool", lambda e: e.affine_select(out=ap, in_=ap, pattern=pattern, compare_op=cmp,
                                                          fill=0.0, base=0, channel_multiplier=cm),
                        reads=W, writes=W)

        def DMA(q, out, in_, R, W, **kw):
            return P.dma(q, lambda e: e.dma_start(out=out, in_=in_, **kw), reads=R, writes=W)

        def MM(items, R, W):
            def f(e):
                r = None
                for (o, l, rr, s0, s1) in items:
                    r = e.matmul(o, lhsT=l, rhs=rr, start=s0, stop=s1)
                return r
            return P.op("pe", f, reads=R, writes=W)

        def TR(items, R, W):
            def f(e):
                r = None
                for (o, i_, idn) in items:
                    r = e.transpose(out=o, in_=i_, identity=idn)
                return r
            return P.op("pe", f, reads=R, writes=W)

        NTOK = S + NS
        identf = sb("identf", [128, 128], F32); B_identf = Buf()
        identb = sb("identb", [128, 128], BF16); B_identb = Buf()
        mixT = sb("mixT", [128, NC8, NTOK], BF16); B_mixT = Buf()
        MARK = cur[0]

        eb = sb("eb", [128, 512], F32); B_eb = Buf()
        enb = sb("enb", [128, 512], F32); B_enb = Buf()
        eD = sb("eD", [128, 512], F32); B_eD = Buf()
        attf = sb("attf", [128, 512], F32); B_attf = Buf()

        zeros = sb("zeros", [128, 256], F32); B_zeros = Buf()
        gmixT = sb("gmixT", [128, NC8], F32); B_gmixT = Buf()
        UU = sb("UU", [128, 128], F32); B_UU = Buf()
        SS = sb("SS", [128, 128], F32); B_SS = Buf()
        cm2 = sb("cm2", [128, 2], F32); B_cm2 = Buf()
        E16 = sb("E16", [16, 16, 128], BF16); B_E16 = Buf()
        MSET(identf[:], 1.0, [B_identf])
        ASEL(identf[:], [[-1, 128]], ALU.is_equal, 1, [B_identf])
        CP(identb[:], identf[:], [B_identf], [B_identb])
        MSET(zeros[:], 0.0, [B_zeros])
        MSET(UU[:], 1.0, [B_UU])
        ASEL(UU[:], [[1, 128]], ALU.is_ge, -1, [B_UU])
        MSET(UU[0:64, 64:128], 0.0, [B_UU])
        MSET(SS[:], 1.0, [B_SS])
        ASEL(SS[:], [[-1, 128]], ALU.is_gt, 1, [B_SS])
        MSET(SS[64:128, 0:64], 0.0, [B_SS])
        MSET(cm2[:], 0.0, [B_cm2])
        MSET(cm2[0:64, 0:1], 1.0, [B_cm2])
        MSET(cm2[64:128, 1:2], 1.0, [B_cm2])
        MSET(E16[:], 1.0, [B_E16])
        ASEL(E16[:], [[-1, 16], [0, 128]], ALU.is_equal, 1, [B_E16])
        DMA("sp", gmixT[:], g_mix.rearrange("(c p) -> p c", p=128), [], [B_gmixT], allow_slow_non_contiguous=True)

        TRI = sb("TRI", [128, 128], BF16); B_TRI = Buf()
        trif = attf[:, 0:128]; B_trif = B_attf
        MSET(trif, 1.0, [B_trif])
        ASEL(trif, [[1, 128]], ALU.is_ge, -1, [B_trif])
        CP(TRI[:], trif, [B_trif], [B_TRI])
        LAM_INIT = 0.8 - 0.6 * float(np.exp(-0.3 * 0))
        lamB = sb("lamB", [128, 4, 64], F32); B_lamB = Buf()
        lamc = sb("lamc", [128, 4], F32); B_lamc = Buf()
        DMA("sp", lamB[:], lam_in.unsqueeze(0).to_broadcast([128, 4, 64]), [], [B_lamB])
        TT(lamB[:, 0, :], lamB[:, 0, :], lamB[:, 1, :], ALU.mult, [B_lamB], [B_lamB])
        TT(lamB[:, 2, :], lamB[:, 2, :], lamB[:, 3, :], ALU.mult, [B_lamB], [B_lamB])
        P.op("dve", lambda e: e.tensor_reduce(out=lamc[:, 0:1], in_=lamB[:, 0, :], axis=AX.X, op=ALU.add),
             reads=[B_lamB], writes=[B_lamc])
        P.op("dve", lambda e: e.tensor_reduce(out=lamc[:, 1:2], in_=lamB[:, 2, :], axis=AX.X, op=ALU.add),
             reads=[B_lamB], writes=[B_lamc])
        ACT(lamc[:, 0:2], lamc[:, 0:2], AF.Exp, [B_lamc], [B_lamc])
        TT(lamc[:, 2:3], lamc[:, 0:1], lamc[:, 1:2], ALU.subtract, [B_lamc], [B_lamc])
        TS(lamc[:, 2:3], lamc[:, 2:3], LAM_INIT, None, ALU.add, None, [B_lamc], [B_lamc])
        TS(lamc[:, 3:4], lamc[:, 2:3], -1.0, None, ALU.mult, None, [B_lamc], [B_lamc])
        gsubB = sb("gsubB", [128, 128], F32); B_gsub = Buf()
        DMA("sp", gsubB[:], g_subln.unsqueeze(0).to_broadcast([128, 128]), [], [B_gsub])
        TS(gsubB[:], gsubB[:], 1.0 - LAM_INIT, None, ALU.mult, None, [B_gsub], [B_gsub])

        l0, l1 = eb, enb
        lbB = sb("lbB", [128, 512], F32); omlB = sb("omlB", [128, 512], F32)
        grecB = sb("grecB", [128, 512], F32)
        B_l0, B_l1, B_lb, B_oml, B_grec = B_eb, B_enb, Buf(), Buf(), Buf()
        DMA("sp", l0[:], hgrn_lb[0:1, :].to_broadcast([128, 512]), [], [B_l0])
        DMA("sp", l1[:], hgrn_lb[1:2, :].to_broadcast([128, 512]), [], [B_l1])
        DMA("sp", grecB[:], g_rec.unsqueeze(0).to_broadcast([128, 512]), [], [B_grec])
        TT(lbB[:], l0[:], l1[:], ALU.max, [B_l0, B_l1], [B_lb])
        TT(l0[:], l0[:], lbB[:], ALU.subtract, [B_l0, B_lb], [B_l0])
        TT(l1[:], l1[:], lbB[:], ALU.subtract, [B_l1, B_lb], [B_l1])
        ACT(l0[:], l0[:], AF.Exp, [B_l0], [B_l0])
        ACT(l1[:], l1[:], AF.Exp, [B_l1], [B_l1])
        TT(l1[:], l0[:], l1[:], ALU.add, [B_l0, B_l1], [B_l1])
        RECIP(l1[:], l1[:], [B_l1], [B_l1])
        TT(lbB[:], l0[:], l1[:], ALU.mult, [B_l0, B_l1], [B_lb])
        TS(omlB[:], lbB[:], -1.0, 1.0, ALU.mult, ALU.add, [B_lb], [B_oml])

        w_sb = sb("w_sb", [128, NC8, INW], BF16)
        B_w = [Buf() for _ in range(NC8)]
        w_v = w_in.rearrange("(c p) n -> p c n", p=128)
        for c in range(NC8):
            DMA("pool", w_sb[:, c, :], w_v[:, c, :], [], [B_w[c]])

        PB = [st.enter_context(nc.psum_tensor("pb%d" % i, [128, 512], F32)) for i in range(7)]
        PT = st.enter_context(nc.psum_tensor("pt", [128, 1024], BF16))
        B_PB = [Buf() for _ in range(7)]
        B_PT = Buf()

        Sst = sb("Sst", [128, 4, 128], F32); B_S = Buf()
        Sb0 = sb("Sb0", [128, 4, 128], BF16); B_Sb0 = Buf()
        Sb1 = sb("Sb1", [128, 4, 128], BF16); B_Sb1 = Buf()
        qT0 = sb("qT0", [128, 4, 128], BF16); B_qT0 = Buf()
        qT1 = sb("qT1", [128, 4, 128], BF16); B_qT1 = Buf()
        MSET(Sst[:], 0.0, [B_S])
        MSET(Sb0[:], 0.0, [B_Sb0])
        MSET(qT0[:], 0.0, [B_qT0])
        MSET(qT1[:], 0.0, [B_qT1])

        KT = sb("KT", [128, 4, S], BF16); B_KT = Buf()
        Vaug = sb("Vaug", [128, NB, 4, 129], BF16); B_Vaug = Buf()
        MSET(Vaug[:, :, :, 128:129], 1.0, [B_Vaug])
        QT = sb("QT", [128, 4, 128], BF16); B_QT = Buf()
        qkb = sb("qkb", [128, 2, 512], BF16); B_qkb = Buf()
        ET = [sb("ET%d" % k, [128, 4, 128], BF16) for k in range(2)]; B_ET = [Buf(), Buf()]
        dens = sb("dens", [128, 16], F32); B_dens = Buf()
        attb = sb("attb", [128, 512], BF16); B_attb = Buf()
        sa = sb("sa", [128, 8], F32); B_sa = Buf()
        attd = sb("attd", [128, 512], F32) if dbg else None; B_attd = Buf()

        def acc_slot(j):
            return PB[4 + j // 3][:, (j % 3) * 129:(j % 3) * 129 + 129], 4 + j // 3

        def subln_and_store_att(nt, tok0):
            for h in range(4):
                ACT(junk[0:nt, 0:128], attf[0:nt, h * 128:(h + 1) * 128], AF.Square, [B_attf], [B_junk, B_sa],
                    accum_out=sa[0:nt, h:h + 1])
            TS(sa[0:nt, 4:8], sa[0:nt, 0:4], 1.0 / 128, EPS, ALU.mult, ALU.add, [B_sa], [B_sa])
            ACT(sa[0:nt, 4:8], sa[0:nt, 4:8], AF.Sqrt, [B_sa], [B_sa])
            RECIP(sa[0:nt, 4:8], sa[0:nt, 4:8], [B_sa], [B_sa])
            for h in range(4):
                STT(attb[0:nt, h * 128:(h + 1) * 128], attf[0:nt, h * 128:(h + 1) * 128], sa[0:nt, 4 + h:5 + h],
                    gsubB[0:nt, :], ALU.mult, ALU.mult, [B_attf, B_sa, B_gsub], [B_attb])
                if dbg:
                    STT(attd[0:nt, h * 128:(h + 1) * 128], attf[0:nt, h * 128:(h + 1) * 128], sa[0:nt, 4 + h:5 + h],
                        gsubB[0:nt, :], ALU.mult, ALU.mult, [B_attf, B_sa, B_gsub], [B_attd])
            if dbg:
                DMA("sp", d_att[tok0:tok0 + nt, :], attd[0:nt, :], [B_attd], [])
            TR([(PT[:, h * 128:h * 128 + nt], attb[0:nt, h * 128:(h + 1) * 128], identb[0:nt, 0:nt])
                for h in range(4)], [B_attb, B_identb], [B_PT])
            CP(mixT[:, 0:4, tok0:tok0 + nt],
               PT[:, 0:512].rearrange("p (h t) -> p h t", h=4)[:, :, 0:nt], [B_PT], [B_mixT])

        NBUF = 2
        xt = [sb("xt%d" % i, [128, D], F32) for i in range(NBUF)]; B_xt = [Buf() for _ in range(NBUF)]
        junk = sb("junk", [128, D], BF16); B_junk = Buf()
        ss = [sb("ss%d" % i, [128, 2], F32) for i in range(NBUF)]; B_ss = [Buf() for _ in range(NBUF)]
        xn = [sb("xn%d" % i, [128, D], BF16) for i in range(NBUF)]; B_xn = [Buf() for _ in range(NBUF)]
        hT = [sb("hT%d" % i, [128, NC8, 128], BF16) for i in range(NBUF)]; B_hT = [Buf() for _ in range(NBUF)]
        kf0 = sb("kf0", [128, 512], F32); kf = [kf0] * NBUF; B_kf0 = Buf(); B_kf = [B_kf0] * NBUF
        vf0 = sb("vf0", [128, 512], F32); vf = [vf0] * NBUF; B_vf0 = Buf(); B_vf = [B_vf0] * NBUF
        Qr = sb("Qr", [128, 512], F32); B_Qr = Buf()
        Ff = sb("Ff", [128, 512], F32); B_Ff = Buf()
        LF = sb("LF", [128, 512], F32); B_LF = Buf()
        KK = sb("KK", [128, 512], F32); B_KK = Buf()
        Vr = sb("Vr", [128, 512], BF16); B_Vr = Buf()
        gG = sb("gG", [128, 512], F32); B_gG = Buf()
        qt = sb("qt", [128, 512], BF16); B_qt = Buf()
        kt = sb("kt", [128, 512], BF16); B_kt = Buf()
        kh = sb("kh", [128, 512], BF16); B_kh = Buf()
        kT = sb("kT", [128, 4, 128], BF16); B_kT = Buf()
        attm = sb("attm", [128, 4, 128], BF16); B_attm = Buf()
        ebl = sb("ebl", [128, 8], F32); B_ebl = Buf()
        rs = sb("rs", [128, 2], F32); B_rs = Buf()
        rec = sb("rec", [128, 512], BF16); B_rec = Buf()
        recf = sb("recf", [128, 512], F32) if dbg else None; B_recf = Buf()

        ones1 = sb("ones1", [128, 1], F32); B_ones1 = Buf()
        MSET(ones1[:], 1.0, [B_ones1])

        blocks = [("p", b, 128) for b in range(NB)] + [("s", 0, NS)]
        zi = [0]
        qs16 = sb("qs16", [16, 512], BF16); B_qs16 = Buf()

        def rstd_from(sumsq, dst, nt, dim, Bsrc, Bdst):
            TS(dst, sumsq, 1.0 / dim, EPS, ALU.mult, ALU.add, [Bsrc], [Bdst])
            ACT(dst, dst, AF.Sqrt, [Bdst], [Bdst])
            RECIP(dst, dst, [Bdst], [Bdst])

        def gate_and_store_rec(o_ps, B_o, nt, tok0, g_rows_B):
            ACT(junk[0:nt, 0:512], o_ps, AF.Square, [B_o], [B_junk, B_rs], accum_out=rs[0:nt, 0:1])
            rstd_from(rs[0:nt, 0:1], rs[0:nt, 1:2], nt, 512, B_rs, B_rs)
            STT(rec[0:nt, :], o_ps, rs[0:nt, 1:2], gG[0:nt, :], ALU.mult, ALU.mult,
                [B_o, B_rs, B_gG], [B_rec])
            if dbg:
                STT(recf[0:nt, :], o_ps, rs[0:nt, 1:2], gG[0:nt, :], ALU.mult, ALU.mult,
                    [B_o, B_rs, B_gG], [B_recf])
                DMA("sp", d_rec[tok0:tok0 + nt, :], recf[0:nt, :], [B_recf], [])
            TR([(PT[:, h * 128:h * 128 + nt], rec[0:nt, h * 128:(h + 1) * 128], identb[0:nt, 0:nt])
                for h in range(4)], [B_rec, B_identb], [B_PT])
            CP(mixT[:, 4:8, tok0:tok0 + nt],
               PT[:, 0:512].rearrange("p (h t) -> p h t", h=4)[:, :, 0:nt], [B_PT], [B_mixT])

        for bi, (kind, b, nt) in enumerate(blocks):
            i = bi % NBUF
            tok0 = b * 128 if kind == "p" else S
            src = x_p[b * 128:(b + 1) * 128, :] if kind == "p" else x_s[0:NS, :]
            DMA("sp", xt[i][0:nt, :], src, [], [B_xt[i]])
            ACT(junk[0:nt, :], xt[i][0:nt, :], AF.Square, [B_xt[i]], [B_junk, B_ss[i]], accum_out=ss[i][0:nt, 0:1])
            rstd_from(ss[i][0:nt, 0:1], ss[i][0:nt, 1:2], nt, D, B_ss[i], B_ss[i])
            ACT(xn[i][0:nt, :], xt[i][0:nt, :], AF.Copy, [B_xt[i], B_ss[i]], [B_xn[i]], scale=ss[i][0:nt, 1:2])
            TR([(PT[:, c * 128:c * 128 + nt], xn[i][0:nt, c * 128:(c + 1) * 128], identb[0:nt, 0:nt])
                for c in range(NC8)], [B_xn[i], B_identb], [B_PT])
            TT(hT[i][:, :, 0:nt], PT[:].rearrange("p (c t) -> p c t", c=NC8)[:, :, 0:nt],
               gmixT[:].unsqueeze(2).to_broadcast([128, NC8, nt]), ALU.mult, [B_PT, B_gmixT], [B_hT[i]])

            def zchunk(n):
                zb = zi[0] % 2
                zi[0] += 1
                MM([(PB[zb][0:nt, :], hT[i][:, c, 0:nt], w_sb[:, c, n * 512:(n + 1) * 512], c == 0, c == NC8 - 1)
                    for c in range(NC8)], [B_hT[i]] + B_w, [B_PB[zb]])
                return PB[zb][0:nt, :], B_PB[zb]

            for n in (1, 2):
                z, Bz = zchunk(n)
                dst, Bd = (kf, B_kf) if n == 1 else (vf, B_vf)
                CP(dst[i][0:nt, :], z, [Bz], [Bd[i]])
                if kind == "p":
                    o = (o_kp if n == 1 else o_vp)[b * 128:(b + 1) * 128, :]
                else:
                    o = (o_ks if n == 1 else o_vs)[0:NS, :]
                DMA("sp", o, dst[i][0:nt, :], [Bd[i]], [])

            if kind == "p":
                z, Bz = zchunk(0)
                CP(qkb[:, 0, :], z, [Bz], [B_qkb])
                CP(qkb[:, 1, :], kf[i][:, :], [B_kf[i]], [B_qkb])
                CP(Vaug[:, b, :, 0:128], vf[i][:, :].rearrange("p (h e) -> p h e", h=4), [B_vf[i]], [B_Vaug])
                TR([(PT[:, (j * 4 + h) * 128:(j * 4 + h + 1) * 128], qkb[:, j, h * 128:(h + 1) * 128], identb[:])
                    for j in range(2) for h in range(4)], [B_qkb, B_identb], [B_PT])
                PT8 = PT[:].rearrange("p (j t) -> p j t", j=8)
                CP(QT[:], PT8[:, 0:4, :], [B_PT], [B_QT])
                CP(KT[:, :, tok0:tok0 + 128], PT8[:, 4:8, :], [B_PT], [B_KT])
                for kb in range(b + 1):
                    for m in range(2):
                        MM([(PB[2 + m][:, h * 128:(h + 1) * 128],
                             KT[m * 64:(m + 1) * 64, h, kb * 128:(kb + 1) * 128],
                             QT[m * 64:(m + 1) * 64, h, :], True, True)
                            for h in range(4)], [B_KT, B_QT], [B_PB[2 + m]])
                        ACT(ET[m][:], PB[2 + m][:, :].rearrange("p (h q) -> p h q", h=4), AF.Exp,
                            [B_PB[2 + m]], [B_ET[m]], scale=64 ** -0.5)
                        if kb == b:
                            TT(ET[m][:], ET[m][:], TRI[:].unsqueeze(1).to_broadcast([128, 4, 128]), ALU.mult,
                               [B_ET[m], B_TRI], [B_ET[m]])
                    items = []
                    for j in range(8):
                        h, m = j // 2, j % 2
                        o_, bk = acc_slot(j)
                        last_in_bank = (j % 3 == 2) or (j == 7)
                        items.append((o_, ET[m][:, h, :], Vaug[:, kb, h, :],
                                      kb == 0 and j % 3 == 0, kb == b and last_in_bank))
                    MM(items, [B_ET[0], B_ET[1], B_Vaug], [B_PB[4], B_PB[5], B_PB[6]])
                for bk in range(3):
                    n_ = 3 if bk < 2 else 2
                    CP(dens[:, 3 * bk:3 * bk + n_],
                       PB[4 + bk][:, 0:n_ * 129].rearrange("p (j c) -> p j c", c=129)[:, :, 128], [B_PB[4 + bk]], [B_dens])
                RECIP(dens[:, 8:16], dens[:, 0:8], [B_dens], [B_dens])
                TS(dens[:, 8:16].rearrange("p (h m) -> p h m", m=2)[:, :, 1],
                   dens[:, 8:16].rearrange("p (h m) -> p h m", m=2)[:, :, 1], lamc[:, 3:4], None, ALU.mult, None,
                   [B_dens, B_lamc], [B_dens])
                for h in range(4):
                    o0, b0 = acc_slot(2 * h)
                    o1, b1 = acc_slot(2 * h + 1)
                    TS(attf[:, h * 128:(h + 1) * 128], o0[:, 0:128], dens[:, 8 + 2 * h:9 + 2 * h], None, ALU.mult, None,
                       [B_PB[b0], B_dens], [B_attf])
                    STT(attf[:, h * 128:(h + 1) * 128], o1[:, 0:128], dens[:, 9 + 2 * h:10 + 2 * h],
                        attf[:, h * 128:(h + 1) * 128], ALU.mult, ALU.add, [B_PB[b1], B_dens, B_attf], [B_attf])
                subln_and_store_att(128, tok0)

            if kind == "s":
                z, Bz = zchunk(0)
                CP(qs16[0:nt, :], z, [Bz], [B_qs16])

            z, Bz = zchunk(3)
            ACT(Qr[0:nt, :], z, AF.Silu, [Bz], [B_Qr])
            z, Bz = zchunk(4)
            ACT(Ff[0:nt, :], z, AF.Sigmoid, [Bz], [B_Ff])
            TT(Ff[0:nt, :], Ff[0:nt, :], omlB[0:nt, :], ALU.mult, [B_Ff, B_oml], [B_Ff])
            TT(Ff[0:nt, :], Ff[0:nt, :], lbB[0:nt, :], ALU.add, [B_Ff, B_lb], [B_Ff])
            ACT(LF[0:nt, :], Ff[0:nt, :], AF.Ln, [B_Ff], [B_LF])
            TS(KK[0:nt, :], Ff[0:nt, :], -1.0, 1.0, ALU.mult, ALU.add, [B_Ff], [B_KK])
            z, Bz = zchunk(5)
            CP(Vr[0:nt, :], z, [Bz], [B_Vr])
            z, Bz = zchunk(6)
            ACT(gG[0:nt, :], z, AF.Silu, [Bz], [B_gG])
            TT(gG[0:nt, :], gG[0:nt, :], grecB[0:nt, :], ALU.mult, [B_gG, B_grec], [B_gG])

            if kind == "p":
                MM([(PB[2][:, :], UU[:], LF[:], True, True)], [B_UU, B_LF], [B_PB[2]])
                MM([(PB[3][:, :], SS[:], LF[:], True, True)], [B_SS, B_LF], [B_PB[3]])
                MM([(PB[5][:, 2 * h:2 * h + 2], LF[:, h * 128:(h + 1) * 128], cm2[:], True, True)
                    for h in range(4)], [B_LF, B_cm2], [B_PB[5]])
                ACT(eb[:], PB[2][:, :], AF.Exp, [B_PB[2]], [B_eb])
                ACT(enb[:], PB[2][:, :], AF.Exp, [B_PB[2]], [B_enb], scale=-1.0)
                ACT(eD[:], PB[3][:, :], AF.Exp, [B_PB[3]], [B_eD])
                ACT(ebl[:], PB[5][:, 0:8], AF.Exp, [B_PB[5]], [B_ebl])
                TT(qt[:], Qr[:], eb[:], ALU.mult, [B_Qr, B_eb], [B_qt])
                TT(kt[:], KK[:], enb[:], ALU.mult, [B_KK, B_enb], [B_kt])
                TT(kh[:], KK[:], eD[:], ALU.mult, [B_KK, B_eD], [B_kh])
                TR([(PT[:, h * 128:(h + 1) * 128], qt[:, h * 128:(h + 1) * 128], identb[:]) for h in range(4)] +
                   [(PT[:, (4 + h) * 128:(5 + h) * 128], kt[:, h * 128:(h + 1) * 128], identb[:]) for h in range(4)],
                   [B_qt, B_kt, B_identb], [B_PT])
                PT4 = PT[:].rearrange("p (j t) -> p j t", j=8)
                CP(qT0[:, :, 0:64], PT4[:, 0:4, 0:64], [B_PT], [B_qT0])
                CP(qT1[:, :, 64:128], PT4[:, 0:4, 64:128], [B_PT], [B_qT1])
                CP(kT[:], PT4[:, 4:8, :], [B_PT], [B_kT])
                MM([(PB[4][:, h * 128:h * 128 + 64], kT[:, h, :], qT0[:, h, 0:64], True, True) for h in range(4)] +
                   [(PB[4][:, h * 128 + 64:(h + 1) * 128], kT[:, h, :], qT1[:, h, 64:128], True, True) for h in range(4)],
                   [B_kT, B_qT0, B_qT1], [B_PB[4]])
                TT(attm[:], PB[4][:, :].rearrange("p (h t) -> p h t", h=4),
                   UU[:].unsqueeze(1).to_broadcast([128, 4, 128]), ALU.mult, [B_PB[4], B_UU], [B_attm])
                for c in range(2):
                    MM([(PB[2 + c][:, h * 128:(h + 1) * 128], kh[c * 64:(c + 1) * 64, h * 128:(h + 1) * 128],
                         Vr[c * 64:(c + 1) * 64, h * 128:(h + 1) * 128], True, True) for h in range(4)],
                       [B_kh, B_Vr], [B_PB[2 + c]])
                for h in range(4):
                    STT(Sst[:, h, :], Sst[:, h, :], ebl[:, 2 * h:2 * h + 1], PB[2][:, h * 128:(h + 1) * 128],
                        ALU.mult, ALU.add, [B_S, B_ebl, B_PB[2]], [B_S])
                CP(Sb1[:], Sst[:], [B_S], [B_Sb1], eng="act")
                MM([t for h in range(4) for t in (
                    (PB[6][:, h * 128:(h + 1) * 128], attm[:, h, :], Vr[:, h * 128:(h + 1) * 128], True, False),
                    (PB[6][:, h * 128:(h + 1) * 128], qT0[:, h, :], Sb0[:, h, :], False, False),
                    (PB[6][:, h * 128:(h + 1) * 128], qT1[:, h, :], Sb1[:, h, :], False, True))],
                   [B_attm, B_Vr, B_qT0, B_qT1, B_Sb0, B_Sb1], [B_PB[6]])
                for h in range(4):
                    STT(Sst[:, h, :], Sst[:, h, :], ebl[:, 2 * h + 1:2 * h + 2], PB[3][:, h * 128:(h + 1) * 128],
                        ALU.mult, ALU.add, [B_S, B_ebl, B_PB[3]], [B_S])
                CP(Sb0[:], Sst[:], [B_S], [B_Sb0], eng="act")
                gate_and_store_rec(PB[6][:, :], B_PB[6], 128, tok0, B_gG)
                if b == NB - 1:
                    DMA("sp", o_sp.rearrange("(h d) v -> d h v", h=4), Sst[:], [B_S], [])
            else:
                fqk = sb("fqk", [128, 3, 4, 16], F32); B_fqk = Buf()
                TR([(PB[4][:, (j * 4 + h) * 16:(j * 4 + h) * 16 + nt], srcT[0:nt, h * 128:(h + 1) * 128], identf[0:nt, 0:nt])
                    for j, srcT in enumerate((Ff, KK, Qr)) for h in range(4)],
                   [B_Ff, B_KK, B_Qr, B_identf], [B_PB[4]])
                CP(fqk[:, :, :, 0:nt], PB[4][:, 0:192].rearrange("p (j h s) -> p j h s", j=3, h=4)[:, :, :, 0:nt],
                   [B_PB[4]], [B_fqk])
                oT = sb("oT", [128, 4, 16], F32); B_oT = Buf()
                v4 = lambda t: t[:].rearrange("p (h v) -> p h v", h=4)
                Ss = [v4(eb), v4(enb)]; B_Ss = [B_eb, B_enb]
                tmpv = v4(eD); B_tmpv = B_eD
                st_v = state_s.rearrange("(s h d) v -> s d h v", h=4, d=128)
                os_v = o_ss.rearrange("(s h d) v -> s d h v", h=4, d=128)
                for s_ in range(nt):
                    k2 = s_ % 2
                    DMA("sp", Ss[k2], st_v[s_], [], [B_Ss[k2]])
                    MM([(PB[5][:, :], E16[0:nt, s_, :], Vr[0:nt, :], True, True)], [B_E16, B_Vr], [B_PB[5]])
                    for h in range(4):
                        TS(tmpv[:, h, :], PB[5][:, h * 128:(h + 1) * 128], fqk[:, 1, h, s_:s_ + 1], None, ALU.mult, None,
                           [B_PB[5], B_fqk], [B_tmpv])
                        STT(Ss[k2][:, h, :], Ss[k2][:, h, :], fqk[:, 0, h, s_:s_ + 1], tmpv[:, h, :],
                            ALU.mult, ALU.add, [B_Ss[k2], B_fqk, B_tmpv], [B_Ss[k2]])
                    DMA("sp", os_v[s_], Ss[k2], [B_Ss[k2]], [])
                    MM([(PB[6][:, h * 16 + s_:h * 16 + s_ + 1], Ss[k2][:, h, :], fqk[:, 2, h, s_:s_ + 1], True, True)
                        for h in range(4)], [B_Ss[k2], B_fqk], [B_PB[6]])
                CP(oT[:, :, 0:nt], PB[6][:, 0:64].rearrange("p (h s) -> p h s", h=4)[:, :, 0:nt], [B_PB[6]], [B_oT])
                TR([(PB[3][0:nt, h * 128:(h + 1) * 128], oT[:, h, 0:nt], identf[:]) for h in range(4)],
                   [B_oT, B_identf], [B_PB[3]])
                gate_and_store_rec(PB[3][0:nt, :], B_PB[3], nt, tok0, B_gG)

        NJ = NPG + 1
        ptB = sb("ptB", [128, NS * NPG], I32); B_ptB = Buf()
        idx = sb("idx", [128, NS * NPG], I32); B_idx = Buf()
        DMA("sp", ptB[:], page_tab.unsqueeze(0).to_broadcast([128, NS * NPG]), [], [B_ptB])
        P.op("pool", lambda e: e.iota(idx[:], pattern=[[0, NS * NPG]], base=0, channel_multiplier=1), writes=[B_idx])
        TS(ptB[:], ptB[:], 7, None, ALU.logical_shift_left, None, [B_ptB], [B_ptB])
        TT(idx[:], idx[:], ptB[:], ALU.bitwise_or, [B_idx, B_ptB], [B_idx])
        coef = sb("coef", [128, 3], F32); B_coef = Buf()
        MSET(coef[:, 1:3], 1.0, [B_coef])
        P.op("pool", lambda e: e.affine_select(out=coef[:, 1:2], in_=coef[:, 1:2], pattern=[[0, 1]],
                                               compare_op=ALU.is_equal, fill=0.0, base=0, channel_multiplier=1),
             reads=[B_coef], writes=[B_coef])
        P.op("pool", lambda e: e.affine_select(out=coef[:, 2:3], in_=coef[:, 2:3], pattern=[[0, 1]],
                                               compare_op=ALU.is_equal, fill=0.0, base=-1, channel_multiplier=1),
             reads=[B_coef], writes=[B_coef])
        STT(coef[:, 0:1], coef[:, 2:3], lamc[:, 3:4], coef[:, 1:2], ALU.mult, ALU.add,
            [B_coef, B_lamc], [B_coef])
        SEL = sb("SEL", [2, 16, 16], F32); B_SEL = Buf()
        MSET(SEL[:], 1.0, [B_SEL])
        P.op("pool", lambda e: e.affine_select(out=SEL[:], in_=SEL[:], pattern=[[1, 16], [-1, 16]],
                                               compare_op=ALU.is_equal, fill=0.0, base=0, channel_multiplier=0),
             reads=[B_SEL], writes=[B_SEL])
        Kp = [eb, enb]; B_Kp = [B_eb, B_enb]
        Vp = [eD, LF]; B_Vp = [B_eD, B_LF]
        prod = KK; B_prod = B_KK
        qB = Ff; B_qB = B_Ff
        Kx = Qr; B_Kx = B_Qr
        Vx = gG; B_Vx = B_gG
        MSET(Kx[:], 0.0, [B_Kx])
        MSET(Vx[:], 0.0, [B_Vx])
        Vpb = [sb("Vpb%d" % k, [128, 4, 129], BF16) for k in range(2)]; B_Vpb = [Buf(), Buf()]
        for k in range(2):
            MSET(Vpb[k][:, :, 128:129], 1.0, [B_Vpb[k]])
        sc = sb("sc", [128, NJ, 8], F32); B_sc = [Buf() for _ in range(NJ)]
        Es = sb("Es", [128, NJ, 8], BF16); B_Es = [Buf() for _ in range(NJ)]
        MSET(Es[:, NPG, :], 0.0, [B_Es[NPG]])
        numN = sb("numN", [2, 512], F32); B_numN = Buf()
        rdc = sb("rdc", [2, 8], F32); B_rdc = Buf()
        scale = 64 ** -0.5
        for s_ in range(NS):
            MM([(PB[0][:, :], E16[0:NS, s_, :], qs16[0:NS, :], True, True)], [B_E16, B_qs16], [B_PB[0]])
            CP(qB[:], PB[0][:, :], [B_PB[0]], [B_qB], eng="act")
            DMA("sp", Kx[0:1, :], kf0[s_:s_ + 1, :], [B_kf0], [B_Kx])
            DMA("sp", Vx[0:1, :], vf0[s_:s_ + 1, :], [B_vf0], [B_Vx])
            for j in range(NJ):
                k2 = (s_ * NJ + j) % 2
                if j < NPG:
                    col = s_ * NPG + j
                    P.dma("pool", (lambda k2, col: lambda e: e.indirect_dma_start(
                        out=Kp[k2][:], out_offset=None, in_=cache_k,
                        in_offset=bass.IndirectOffsetOnAxis(ap=idx[:, col:col + 1], axis=0)))(k2, col), reads=[B_idx], writes=[B_Kp[k2]])
                    P.dma("pool", (lambda k2, col: lambda e: e.indirect_dma_start(
                        out=Vp[k2][:], out_offset=None, in_=cache_v,
                        in_offset=bass.IndirectOffsetOnAxis(ap=idx[:, col:col + 1], axis=0)))(k2, col), reads=[B_idx], writes=[B_Vp[k2]])
                    Ksrc, BK, Vsrc, BV = Kp[k2], B_Kp[k2], Vp[k2], B_Vp[k2]
                else:
                    Ksrc, BK, Vsrc, BV = Kx, B_Kx, Vx, B_Vx
                TT(prod[:], Ksrc[:], qB[:], ALU.mult, [BK, B_qB], [B_prod], eng="pool")
                P.op("dve", (lambda j: lambda e: e.tensor_reduce(
                    out=sc[:, j, :], in_=prod[:].rearrange("p (g d) -> p g d", d=64), axis=AX.X, op=ALU.add))(j),
                    reads=[B_prod], writes=[B_sc[j]])
                if j < NPG:
                    ACT(Es[:, j, :], sc[:, j, :], AF.Exp, [B_sc[j]], [B_Es[j]], scale=scale)
                else:
                    ACT(Es[0:1, j, :], sc[0:1, j, :], AF.Exp, [B_sc[j]], [B_Es[j]], scale=scale)
                CP(Vpb[k2][:, :, 0:128], Vsrc[:].rearrange("p (h e) -> p h e", h=4), [BV], [B_Vpb[k2]], eng="act")
                MM([(PB[1 + h // 3][0:2, (h % 3) * 129:(h % 3) * 129 + 129], Es[:, j, 2 * h:2 * h + 2], Vpb[k2][:, h, :],
                     j == 0 and h % 3 == 0, j == NJ - 1 and h in (2, 3)) for h in range(4)],
                   [B_Es[j], B_Vpb[k2]], [B_PB[1], B_PB[2]])
            for bk, hs in ((1, (0, 1, 2)), (2, (3,))):
                n_ = len(hs)
                CP(rdc[:, hs[0]:hs[0] + n_], PB[bk][0:2, 0:n_ * 129].rearrange("p (j c) -> p j c", c=129)[:, :, 128],
                   [B_PB[bk]], [B_rdc])
            RECIP(rdc[:, 4:8], rdc[:, 0:4], [B_rdc], [B_rdc])
            TS(rdc[:, 4:8], rdc[:, 4:8], coef[0:2, 0:1], None, ALU.mult, None, [B_rdc, B_coef], [B_rdc])
            for h in range(4):
                TS(numN[:, h * 128:(h + 1) * 128], PB[1 + h // 3][0:2, (h % 3) * 129:(h % 3) * 129 + 128],
                   rdc[:, 4 + h:5 + h], None, ALU.mult, None, [B_PB[1 + h // 3], B_rdc], [B_numN])
            MM([(PB[3][0:NS, :], SEL[:, s_, 0:NS], numN[:, :], s_ == 0, s_ == NS - 1)], [B_SEL, B_numN], [B_PB[3]])
        CP(attf[0:NS, :], PB[3][0:NS, :], [B_PB[3]], [B_attf])
        subln_and_store_att(NS, S)

        P.barrier()
        cur[0] = MARK
        NBLK = NB + 1
        blkinfo = [(x_p[b * 128:(b + 1) * 128, :], p_p[b * 128:(b + 1) * 128, :], o_yp[b * 128:(b + 1) * 128, :], 128, b * 128)
                   for b in range(NB)] + [(x_s[0:NS, :], p_s[0:NS, :], o_ys[0:NS, :], NS, S)]
        x1 = sb("x1", [128, NBLK, D], F32); B_x1 = [Buf() for _ in range(NBLK)]
        hT2 = mixT; B_hT2 = B_mixT
        wo_sb = sb("wo_sb", [128, NC8, D], BF16); B_wo = Buf()
        wg_sb = sb("wg_sb", [128, NC8, D], BF16); B_wg = Buf()
        wp_sb = sb("wp_sb", [128, 2, D], BF16); B_wp = Buf()
        FS = 512
        NSL = 4096 // FS
        wu = [sb("wu%d" % k, [128, NC8, FS], BF16) for k in range(2)]; B_wu = [Buf(), Buf()]
        wd = [sb("wd%d" % k, [128, FS // 128, D], BF16) for k in range(2)]; B_wd = [Buf(), Buf()]
        actT = [sb("actT%d" % k, [128, FS // 128, 512], BF16) for k in range(2)]; B_actT = [Buf(), Buf()]
        cxt = [sb("cxt%d" % k, [128, D], F32) for k in range(2)]; B_cxt = [Buf(), Buf()]
        cjunk = sb("cjunk", [128, D], BF16); B_cjunk = Buf()
        cxn = sb("cxn", [128, D], BF16); B_cxn = Buf()
        css = sb("css", [128, 2], F32); B_css = Buf()
        gffnT = sb("gffnT", [128, NC8], F32); B_gffnT = Buf()
        gfinB = sb("gfinB", [128, D], F32); B_gfin = Buf()
        x2T = sb("x2T", [128, NC8, 128], BF16); B_x2T = Buf()
        sg = sb("sg", [128, D], F32); B_sg = Buf()
        pf = sb("pf", [128, 256], F32); B_pf = Buf()
        pb16 = sb("pb16", [128, 256], BF16); B_pb16 = Buf()
        pT = sb("pT", [128, 2, 128], BF16); B_pT = Buf()
        yt = sb("yt", [128, D], F32); B_yt = Buf()

        DMA("pool", wo_sb[:], w_out.rearrange("(c p) n -> p c n", p=128), [], [B_wo])
        DMA("sp", gffnT[:], g_ffn.rearrange("(c p) -> p c", p=128), [], [B_gffnT], allow_slow_non_contiguous=True)
        DMA("sp", gfinB[:], g_final.unsqueeze(0).to_broadcast([128, D]), [], [B_gfin])

        def load_slice(sl):
            k = sl % 2
            DMA("pool", wu[k][:], w_up[:, sl * FS:(sl + 1) * FS].rearrange("(c p) f -> p c f", p=128), [], [B_wu[k]])
            DMA("pool", wd[k][:], w_down[sl * FS:(sl + 1) * FS, :].rearrange("(fc p) n -> p fc n", p=128), [], [B_wd[k]])

        for bi, (xsrc, psrc, ydst, nt, tok0) in enumerate(blkinfo):
            k = bi % 2
            DMA("sp", cxt[k][0:nt, :], xsrc, [], [B_cxt[k]])
            for nh in range(2):
                MM([(PB[nh][0:nt, :], mixT[:, c, tok0:tok0 + nt], wo_sb[:, c, nh * 512:(nh + 1) * 512], c == 0, c == NC8 - 1)
                    for c in range(NC8)], [B_mixT, B_wo], [B_PB[nh]])
                TT(x1[0:nt, bi, nh * 512:(nh + 1) * 512], cxt[k][0:nt, nh * 512:(nh + 1) * 512], PB[nh][0:nt, :], ALU.add,
                   [B_cxt[k], B_PB[nh]], [B_x1[bi]])
            ACT(cjunk[0:nt, :], x1[0:nt, bi, :], AF.Square, [B_x1[bi]], [B_cjunk, B_css], accum_out=css[0:nt, 0:1])
            rstd_from(css[0:nt, 0:1], css[0:nt, 1:2], nt, D, B_css, B_css)
            ACT(cxn[0:nt, :], x1[0:nt, bi, :], AF.Copy, [B_x1[bi], B_css], [B_cxn], scale=css[0:nt, 1:2])
            TR([(PT[:, c * 128:c * 128 + nt], cxn[0:nt, c * 128:(c + 1) * 128], identb[0:nt, 0:nt])
                for c in range(NC8)], [B_cxn, B_identb], [B_PT])
            TT(hT2[:, :, tok0:tok0 + nt], PT[:].rearrange("p (c t) -> p c t", c=NC8)[:, :, 0:nt],
               gffnT[:].unsqueeze(2).to_broadcast([128, NC8, nt]), ALU.mult, [B_PT, B_gffnT], [B_hT2])
            if bi == 0:
                load_slice(0)

        groups = [(g * 512, 512, list(range(g * 4, g * 4 + 4))) for g in range(S // 512)]
        if S % 512:
            g0 = (S // 512) * 512
            groups.append((g0, S - g0, list(range(g0 // 128, NB))))
        groups.append((S, NS, [NB]))
        ai = [0]
        for sl in range(NSL):
            k = sl % 2
            if sl + 1 < NSL:
                load_slice(sl + 1)
            for (t0, ntk, blks) in groups:
                a = ai[0] % 2
                ai[0] += 1
                for fc in range(FS // 128):
                    MM([(PB[fc][:, 0:ntk], wu[k][:, c, fc * 128:(fc + 1) * 128], hT2[:, c, t0:t0 + ntk], c == 0, c == NC8 - 1)
                        for c in range(NC8)], [B_wu[k], B_hT2], [B_PB[fc]])
                    ACT(actT[a][:, fc, 0:ntk], PB[fc][:, 0:ntk], AF.Relu, [B_PB[fc]], [B_actT[a]])
                TT(actT[a][:, :, 0:ntk], actT[a][:, :, 0:ntk], actT[a][:, :, 0:ntk], ALU.mult, [B_actT[a]], [B_actT[a]],
                   eng="pool")
                for bj, blk in enumerate(blks):
                    nt = blkinfo[blk][3]
                    for nh in range(2):
                        MM([(PB[4 + nh][0:nt, :], actT[a][:, fc, bj * 128:bj * 128 + nt], wd[k][:, fc, nh * 512:(nh + 1) * 512],
                             fc == 0, fc == FS // 128 - 1) for fc in range(FS // 128)], [B_actT[a], B_wd[k]], [B_PB[4 + nh]])
                        TT(x1[0:nt, blk, nh * 512:(nh + 1) * 512], x1[0:nt, blk, nh * 512:(nh + 1) * 512], PB[4 + nh][0:nt, :],
                           ALU.add, [B_x1[blk], B_PB[4 + nh]], [B_x1[blk]])
            if sl == 0:
                DMA("pool", wg_sb[:], w_gate.rearrange("(c p) n -> p c n", p=128), [], [B_wg])
                DMA("pool", wp_sb[:], w_proj.rearrange("(c p) n -> p c n", p=128), [], [B_wp])

        for bi, (xsrc, psrc, ydst, nt, tok0) in enumerate(blkinfo):
            DMA("sp", pf[0:nt, :], psrc, [], [B_pf])
            CP(pb16[0:nt, :], pf[0:nt, :], [B_pf], [B_pb16])
            ACT(cxn[0:nt, :], x1[0:nt, bi, :], AF.Copy, [B_x1[bi]], [B_cxn])
            TR([(PT[:, c * 128:c * 128 + nt], cxn[0:nt, c * 128:(c + 1) * 128], identb[0:nt, 0:nt])
                for c in range(NC8)], [B_cxn, B_identb], [B_PT])
            CP(x2T[:, :, 0:nt], PT[:].rearrange("p (c t) -> p c t", c=NC8)[:, :, 0:nt], [B_PT], [B_x2T])
            TR([(PT[:, c * 128:c * 128 + nt], pb16[0:nt, c * 128:(c + 1) * 128], identb[0:nt, 0:nt])
                for c in range(2)], [B_pb16, B_identb], [B_PT])
            CP(pT[:, :, 0:nt], PT[:, 0:256].rearrange("p (c t) -> p c t", c=2)[:, :, 0:nt], [B_PT], [B_pT])
            for nh in range(2):
                MM([(PB[nh][0:nt, :], x2T[:, c, 0:nt], wg_sb[:, c, nh * 512:(nh + 1) * 512], c == 0, c == NC8 - 1)
                    for c in range(NC8)], [B_x2T, B_wg], [B_PB[nh]])
                ACT(sg[0:nt, nh * 512:(nh + 1) * 512], PB[nh][0:nt, :], AF.Sigmoid, [B_PB[nh]], [B_sg])
                MM([(PB[2 + nh][0:nt, :], pT[:, c, 0:nt], wp_sb[:, c, nh * 512:(nh + 1) * 512], c == 0, c == 1)
                    for c in range(2)], [B_pT, B_wp], [B_PB[2 + nh]])
                TT(sg[0:nt, nh * 512:(nh + 1) * 512], sg[0:nt, nh * 512:(nh + 1) * 512], PB[2 + nh][0:nt, :], ALU.mult,
                   [B_sg, B_PB[2 + nh]], [B_sg])
            TT(x1[0:nt, bi, :], x1[0:nt, bi, :], sg[0:nt, :], ALU.add, [B_x1[bi], B_sg], [B_x1[bi]])
            ACT(cjunk[0:nt, :], x1[0:nt, bi, :], AF.Square, [B_x1[bi]], [B_cjunk, B_css], accum_out=css[0:nt, 0:1])
            rstd_from(css[0:nt, 0:1], css[0:nt, 1:2], nt, D, B_css, B_css)
            STT(yt[0:nt, :], x1[0:nt, bi, :], css[0:nt, 1:2], gfinB[0:nt, :], ALU.mult, ALU.mult,
                [B_x1[bi], B_css, B_gfin], [B_yt])
            DMA("sp", ydst, yt[0:nt, :], [B_yt], [])

        P.emit(st)
    return nc


def kernel(x_prompt, x_sample, cache_k, cache_v, state_hgrn, page_table, p_prompt, p_sample,
           w_in, lambda_q1, lambda_k1, lambda_q2, lambda_k2, g_subln, hgrn_lb, g_rec, w_out,
           g_mix, g_ffn, w_up, w_down, w_ple_gate, w_ple_proj, g_final):
    n = 8
    B, S, _ = x_prompt.shape
    NSA = x_sample.shape[0]
    NS = NSA // n
    f = lambda a: np.ascontiguousarray(np.asarray(a, dtype=np.float32))
    NPOOL = cache_k.shape[1]
    NPG = page_table.shape[1]
    nc = build_nc(S=S, NS=NS, NPG=NPG, NPOOL=NPOOL)
    ck = f(cache_k[0]).reshape(NPOOL * 128, 512)
    cv = f(cache_v[0]).reshape(NPOOL * 128, 512)
    in_maps = []
    for c in range(n):
        in_maps.append({
            "x_p": f(x_prompt[c]),
            "x_s": f(x_sample[c * NS:(c + 1) * NS, 0]),
            "w_in": f(w_in[0]),
            "g_mix": f(g_mix[0]),
            "hgrn_lb": f(hgrn_lb),
            "lam_in": f(np.concatenate([lambda_q1, lambda_k1, lambda_q2, lambda_k2], 0)),
            "g_subln": f(g_subln[0]),
            "w_out": f(w_out[0]), "g_ffn": f(g_ffn[0]), "w_up": f(w_up[0]), "w_down": f(w_down[0]),
            "w_gate": f(w_ple_gate[0]), "w_proj": f(w_ple_proj[0]),
            "p_p": f(p_prompt[0, c]), "p_s": f(p_sample[0, c * NS:(c + 1) * NS, 0]),
            "g_final": f(g_final),
            "cache_k": ck, "cache_v": cv,
            "page_tab": np.ascontiguousarray(np.asarray(page_table[c * NS:(c + 1) * NS], dtype=np.int32).reshape(-1)),
            "g_rec": f(g_rec[0]),
            "state_s": f(state_hgrn[0, c * NS:(c + 1) * NS]).reshape(NS * 4 * 128, 128),
        })
    res = run_bass_kernel_spmd(nc, in_maps, core_ids=list(range(n))).results
    cat = lambda k: np.stack([r[k] for r in res], 0)
    y_prompt = cat("o_yp").reshape(B, S, D)
    y_sample = cat("o_ys").reshape(NSA, 1, D)
    k_prompt = cat("o_kp").reshape(1, B, S, 4, 128)
    v_prompt = cat("o_vp").reshape(1, B, S, 4, 128)
    s_prompt = cat("o_sp").reshape(1, B, 4, 128, 128)
    k_sample = cat("o_ks").reshape(1, NSA, 1, 4, 128)
    v_sample = cat("o_vs").reshape(1, NSA, 1, 4, 128)
    s_sample = cat("o_ss").reshape(1, NSA, 4, 128, 128)
    return tuple(np.ascontiguousarray(a, dtype=np.float32) for a in
                 (y_prompt, y_sample, k_prompt, v_prompt, s_prompt, k_sample, v_sample, s_sample))
```

```python
from contextlib import ExitStack

import numpy as np
import concourse.bass as bass
import concourse.mybir as mybir
from concourse.bass_utils import run_bass_kernel_spmd

F32 = mybir.dt.float32
BF16 = mybir.dt.bfloat16
I32 = mybir.dt.int32
AF = mybir.ActivationFunctionType
ALU = mybir.AluOpType
AX = mybir.AxisListType

D = 1024
NC8 = 8
INW = 3584
EPS = 1e-6
COMPUTE = ("pe", "act", "dve", "pool")


class Ins:
    __slots__ = ("eng", "fn", "deps", "inc", "sem", "val", "is_dma")

    def __init__(self, eng, fn, is_dma=False):
        self.eng = eng
        self.fn = fn
        self.deps = []
        self.inc = False
        self.sem = None
        self.val = None
        self.is_dma = is_dma


class Buf:
    __slots__ = ("name", "w", "r")

    def __init__(self, name=""):
        self.name = name
        self.w = []
        self.r = []


class Prog:
    def __init__(self, nc, ndma=None):
        self.nc = nc
        self.q = {e: [] for e in ("pe", "act", "dve", "pool", "sp")}
        self.ndma = ndma or {"sp": 32, "pool": 24, "act": 8}
        self.dma_hist = {k: [] for k in self.ndma}
        self.all_dma = []
        self.pending = {}

    def barrier(self):
        deps = []
        for e, q in self.q.items():
            last = None
            for ins in reversed(q):
                if not ins.is_dma:
                    last = ins
                    break
            if last is not None:
                deps.append(last)
        for qn, hist in self.dma_hist.items():
            deps.extend(hist[-self.ndma[qn]:])
        for e in self.q:
            self.pending[e] = list(deps)

    def _track(self, ins, reads, writes, deps):
        ds = []
        for b in reads:
            ds.extend(b.w)
        for b in writes:
            ds.extend(b.w)
            ds.extend(b.r)
        ds.extend(deps)
        seen = set()
        for d in ds:
            if d is None or id(d) in seen or d is ins:
                continue
            seen.add(id(d))
            if (not d.is_dma) and (not ins.is_dma) and d.eng == "pe" and ins.eng == "pe":
                continue
            ins.deps.append(d)
            d.inc = True
        for b in writes:
            b.w = [ins]
            b.r = []
        for b in reads:
            if b not in writes:
                b.r.append(ins)

    def op(self, eng, fn, reads=(), writes=(), deps=()):
        ins = Ins(eng, fn)
        deps = list(deps) + self.pending.pop(eng, [])
        self._track(ins, reads, writes, deps)
        self.q[eng].append(ins)
        return ins

    def dma(self, queue, fn, reads=(), writes=(), deps=()):
        ins = Ins(queue, fn, is_dma=True)
        ins.inc = True
        hist = self.dma_hist[queue]
        n = self.ndma[queue]
        j = len(hist)
        extra = list(deps) + self.pending.pop(queue, [])
        if j >= n:
            extra.append(hist[j - n])
        self._track(ins, reads, writes, extra)
        ins.sem = (queue, j % n)
        ins.val = 16 * (j // n + 1)
        hist.append(ins)
        self.all_dma.append(ins)
        self.q[queue].append(ins)
        return ins

    def emit(self, stack):
        nc = self.nc
        esem = {e: stack.enter_context(nc.semaphore("s_" + e)) for e in COMPUTE}
        dsem = {}
        for qn, n in self.ndma.items():
            for i in range(n):
                dsem[(qn, i)] = stack.enter_context(nc.semaphore("d_%s%d" % (qn, i)))
        for e in ("pe", "act", "dve", "pool", "sp"):
            c = 0
            for ins in self.q[e]:
                if ins.is_dma:
                    ins.sem = dsem[ins.sem]
                elif ins.inc:
                    c += 1
                    ins.sem = esem[e]
                    ins.val = c
        final = {}
        for d in self.all_dma:
            k = d.sem.num
            if k not in final or final[k][1] < d.val:
                final[k] = (d.sem, d.val)
        block = stack.enter_context(nc.Block())

        def run(engname, eng):
            waited = {}
            for ins in self.q[engname]:
                for d in ins.deps:
                    k = d.sem.num
                    if waited.get(k, 0) >= d.val:
                        continue
                    eng.wait_ge(d.sem, d.val)
                    waited[k] = d.val
                bi = ins.fn(eng)
                if ins.inc:
                    bi.then_inc(ins.sem, 16 if ins.is_dma else 1)
            if engname == "sp":
                for k, (s, v) in final.items():
                    if waited.get(k, 0) < v:
                        eng.wait_ge(s, v)

        block.tensor(lambda e: run("pe", e))
        block.scalar(lambda e: run("act", e))
        block.vector(lambda e: run("dve", e))
        block.gpsimd(lambda e: run("pool", e))
        block.sync(lambda e: run("sp", e))


def build_nc(S=2048, NS=16, dbg=False, NPG=16, NPOOL=2560):
    assert S % 128 == 0
    NB = S // 128
    nc = bass.Bass("TRN2", target_bir_lowering=False)

    def din(name, shape, dt=F32):
        return nc.dram_tensor(name, shape, dt, kind="ExternalInput").ap()

    def dout(name, shape, dt=F32):
        return nc.dram_tensor(name, shape, dt, kind="ExternalOutput").ap()

    x_p = din("x_p", [S, D])
    x_s = din("x_s", [NS, D])
    w_in = din("w_in", [D, INW])
    g_mix = din("g_mix", [D])
    hgrn_lb = din("hgrn_lb", [2, 512])
    g_rec = din("g_rec", [512])
    state_s = din("state_s", [NS * 4 * 128, 128])
    lam_in = din("lam_in", [4, 64])
    g_subln = din("g_subln", [128])
    w_out = din("w_out", [D, D])
    g_ffn = din("g_ffn", [D])
    w_up = din("w_up", [D, 4096])
    w_down = din("w_down", [4096, D])
    w_gate = din("w_gate", [D, D])
    w_proj = din("w_proj", [256, D])
    p_p = din("p_p", [S, 256])
    p_s = din("p_s", [NS, 256])
    g_final = din("g_final", [D])
    cache_kv = din("cache_kv", [NPOOL * 128, 1024])
    page_tab = din("page_tab", [NS * NPG], I32)

    o_yp = dout("o_yp", [S, D])
    o_ys = dout("o_ys", [NS, D])
    o_kp = dout("o_kp", [S, 512])
    o_vp = dout("o_vp", [S, 512])
    o_sp = dout("o_sp", [4 * 128, 128])
    o_ks = dout("o_ks", [NS, 512])
    o_vs = dout("o_vs", [NS, 512])
    o_ss = dout("o_ss", [NS * 4 * 128, 128])
    if dbg:
        d_rec = dout("d_rec", [S + NS, 512])
        d_att = dout("d_att", [S + NS, 512])

    SB_LO, SB_HI = 16512, 229344
    cur = [SB_LO]
    DTB = {F32: 4, BF16: 2, I32: 4}

    with ExitStack() as st:
        def sb(name, shape, dt):
            n = DTB[dt]
            for d_ in shape[1:]:
                n *= d_
            off = (cur[0] + 31) // 32 * 32
            if off + n > SB_HI:
                raise AssertionError("SBUF overflow placing %s: need %d at %d (limit %d)" % (name, n, off, SB_HI))
            cur[0] = off + n
            return nc.alloc_sbuf_tensor_at(name, list(shape), dt, offset=off)

        P = Prog(nc)

        def ACT(out, in_, func, R, W, **kw):
            return P.op("act", lambda e: e.activation(out=out, in_=in_, func=func, **kw), reads=R, writes=W)

        def TT(out, in0, in1, op, R, W, eng="dve"):
            return P.op(eng, lambda e: e.tensor_tensor(out=out, in0=in0, in1=in1, op=op), reads=R, writes=W)

        def TS(out, in0, s1, s2, op0, op1, R, W, eng="dve"):
            if op1 is None:
                return P.op(eng, lambda e: e.tensor_scalar(out=out, in0=in0, scalar1=s1, scalar2=None, op0=op0),
                            reads=R, writes=W)
            return P.op(eng, lambda e: e.tensor_scalar(out=out, in0=in0, scalar1=s1, scalar2=s2, op0=op0, op1=op1),
                        reads=R, writes=W)

        def STT(out, in0, scalar, in1, op0, op1, R, W, eng="dve"):
            return P.op(eng, lambda e: e.scalar_tensor_tensor(out=out, in0=in0, scalar=scalar, in1=in1,
                                                              op0=op0, op1=op1), reads=R, writes=W)

        def CP(out, in_, R, W, eng="dve"):
            if eng == "act":
                return P.op("act", lambda e: e.activation(out=out, in_=in_, func=AF.Copy), reads=R, writes=W)
            return P.op(eng, lambda e: e.tensor_copy(out=out, in_=in_), reads=R, writes=W)

        def RECIP(out, in_, R, W):
            return P.op("dve", lambda e: e.reciprocal(out=out, in_=in_), reads=R, writes=W)

        def MSET(ap, v, W, eng="pool"):
            return P.op(eng, lambda e: e.memset(ap, v), writes=W)

        def ASEL(ap, pattern, cmp, cm, W):
            return P.op("pool", lambda e: e.affine_select(out=ap, in_=ap, pattern=pattern, compare_op=cmp,
                                                          fill=0.0, base=0, channel_multiplier=cm),
                        reads=W, writes=W)

        def DMA(q, out, in_, R, W, **kw):
            return P.dma(q, lambda e: e.dma_start(out=out, in_=in_, **kw), reads=R, writes=W)

        def MM(items, R, W):
            def f(e):
                r = None
                for (o, l, rr, s0, s1) in items:
                    r = e.matmul(o, lhsT=l, rhs=rr, start=s0, stop=s1)
                return r
            return P.op("pe", f, reads=R, writes=W)

        def TR(items, R, W):
            def f(e):
                r = None
                for (o, i_, idn) in items:
                    r = e.transpose(out=o, in_=i_, identity=idn)
                return r
            return P.op("pe", f, reads=R, writes=W)

        NTOK = S + NS
        identf = sb("identf", [128, 128], F32); B_identf = Buf()
        identb = sb("identb", [128, 128], BF16); B_identb = Buf()
        mixT = sb("mixT", [128, NC8, NTOK], BF16); B_mixT = Buf()
        MARK = cur[0]

        wk = sb("wk", [128, 8, 512], F32)
        eb, enb, eD, attf = wk[:, 0, :], wk[:, 1, :], wk[:, 2, :], wk[:, 3, :]
        Qr, Ff, LF, KK = wk[:, 4, :], wk[:, 5, :], wk[:, 6, :], wk[:, 7, :]
        B_eb, B_enb, B_eD, B_attf = Buf(), Buf(), Buf(), Buf()
        B_Qr, B_Ff, B_LF, B_KK = Buf(), Buf(), Buf(), Buf()

        gmixT = sb("gmixT", [128, NC8], F32); B_gmixT = Buf()
        UU = sb("UU", [128, 128], F32); B_UU = Buf()
        SS = sb("SS", [128, 128], F32); B_SS = Buf()
        cm2 = sb("cm2", [128, 2], F32); B_cm2 = Buf()
        E16 = sb("E16", [16, 16, 128], BF16); B_E16 = Buf()
        MSET(identf[:], 1.0, [B_identf])
        ASEL(identf[:], [[-1, 128]], ALU.is_equal, 1, [B_identf])
        CP(identb[:], identf[:], [B_identf], [B_identb])
        MSET(UU[:], 1.0, [B_UU])
        ASEL(UU[:], [[1, 128]], ALU.is_ge, -1, [B_UU])
        MSET(UU[0:64, 64:128], 0.0, [B_UU])
        MSET(SS[:], 1.0, [B_SS])
        ASEL(SS[:], [[-1, 128]], ALU.is_gt, 1, [B_SS])
        MSET(SS[64:128, 0:64], 0.0, [B_SS])
        MSET(cm2[:], 0.0, [B_cm2])
        MSET(cm2[0:64, 0:1], 1.0, [B_cm2])
        MSET(cm2[64:128, 1:2], 1.0, [B_cm2])
        MSET(E16[:], 1.0, [B_E16])
        ASEL(E16[:], [[-1, 16], [0, 128]], ALU.is_equal, 1, [B_E16])
        DMA("sp", gmixT[:], g_mix.rearrange("(c p) -> p c", p=128), [], [B_gmixT], allow_slow_non_contiguous=True)

        TRI = sb("TRI", [128, 128], BF16); B_TRI = Buf()
        trif = attf[:, 0:128]; B_trif = B_attf
        MSET(trif, 1.0, [B_trif])
        ASEL(trif, [[1, 128]], ALU.is_ge, -1, [B_trif])
        CP(TRI[:], trif, [B_trif], [B_TRI])
        LAM_INIT = 0.8 - 0.6 * float(np.exp(-0.3 * 0))
        lamB = sb("lamB", [128, 4, 64], F32); B_lamB = Buf()
        lamc = sb("lamc", [128, 4], F32); B_lamc = Buf()
        DMA("sp", lamB[:], lam_in.unsqueeze(0).to_broadcast([128, 4, 64]), [], [B_lamB])
        TT(lamB[:, 0, :], lamB[:, 0, :], lamB[:, 1, :], ALU.mult, [B_lamB], [B_lamB])
        TT(lamB[:, 2, :], lamB[:, 2, :], lamB[:, 3, :], ALU.mult, [B_lamB], [B_lamB])
        P.op("dve", lambda e: e.tensor_reduce(out=lamc[:, 0:1], in_=lamB[:, 0, :], axis=AX.X, op=ALU.add),
             reads=[B_lamB], writes=[B_lamc])
        P.op("dve", lambda e: e.tensor_reduce(out=lamc[:, 1:2], in_=lamB[:, 2, :], axis=AX.X, op=ALU.add),
             reads=[B_lamB], writes=[B_lamc])
        ACT(lamc[:, 0:2], lamc[:, 0:2], AF.Exp, [B_lamc], [B_lamc])
        TT(lamc[:, 2:3], lamc[:, 0:1], lamc[:, 1:2], ALU.subtract, [B_lamc], [B_lamc])
        TS(lamc[:, 2:3], lamc[:, 2:3], LAM_INIT, None, ALU.add, None, [B_lamc], [B_lamc])
        TS(lamc[:, 3:4], lamc[:, 2:3], -1.0, None, ALU.mult, None, [B_lamc], [B_lamc])
        gsubB = sb("gsubB", [128, 128], F32); B_gsub = Buf()
        DMA("sp", gsubB[:], g_subln.unsqueeze(0).to_broadcast([128, 128]), [], [B_gsub])
        TS(gsubB[:], gsubB[:], 1.0 - LAM_INIT, None, ALU.mult, None, [B_gsub], [B_gsub])

        l0, l1 = eb, enb
        lbB = sb("lbB", [128, 512], F32); omlB = sb("omlB", [128, 512], F32)
        grecB = sb("grecB", [128, 512], F32)
        B_l0, B_l1, B_lb, B_oml, B_grec = B_eb, B_enb, Buf(), Buf(), Buf()
        DMA("sp", l0[:], hgrn_lb[0:1, :].to_broadcast([128, 512]), [], [B_l0])
        DMA("sp", l1[:], hgrn_lb[1:2, :].to_broadcast([128, 512]), [], [B_l1])
        DMA("sp", grecB[:], g_rec.unsqueeze(0).to_broadcast([128, 512]), [], [B_grec])
        TT(lbB[:], l0[:], l1[:], ALU.max, [B_l0, B_l1], [B_lb])
        TT(l0[:], l0[:], lbB[:], ALU.subtract, [B_l0, B_lb], [B_l0])
        TT(l1[:], l1[:], lbB[:], ALU.subtract, [B_l1, B_lb], [B_l1])
        ACT(l0[:], l0[:], AF.Exp, [B_l0], [B_l0])
        ACT(l1[:], l1[:], AF.Exp, [B_l1], [B_l1])
        TT(l1[:], l0[:], l1[:], ALU.add, [B_l0, B_l1], [B_l1])
        RECIP(l1[:], l1[:], [B_l1], [B_l1])
        TT(lbB[:], l0[:], l1[:], ALU.mult, [B_l0, B_l1], [B_lb])
        TS(omlB[:], lbB[:], -1.0, 1.0, ALU.mult, ALU.add, [B_lb], [B_oml])

        w_sb = sb("w_sb", [128, NC8, INW], BF16)
        B_w = [Buf() for _ in range(NC8)]
        w_v = w_in.rearrange("(c p) n -> p c n", p=128)
        for c in range(NC8):
            DMA("pool", w_sb[:, c, :], w_v[:, c, :], [], [B_w[c]])

        PB = [st.enter_context(nc.psum_tensor("pb%d" % i, [128, 512], F32)) for i in range(7)]
        PT = st.enter_context(nc.psum_tensor("pt", [128, 1024], BF16))
        B_PB = [Buf() for _ in range(7)]
        B_PT = Buf()

        Sst = sb("Sst", [128, 4, 128], F32); B_S = Buf()
        Sb0 = sb("Sb0", [128, 4, 128], BF16); B_Sb0 = Buf()
        Sb1 = sb("Sb1", [128, 4, 128], BF16); B_Sb1 = Buf()
        qT0 = sb("qT0", [128, 4, 128], BF16); B_qT0 = Buf()
        qT1 = sb("qT1", [128, 4, 128], BF16); B_qT1 = Buf()
        MSET(Sst[:], 0.0, [B_S])
        MSET(Sb0[:], 0.0, [B_Sb0])
        MSET(qT0[:], 0.0, [B_qT0])
        MSET(qT1[:], 0.0, [B_qT1])

        KT = sb("KT", [128, 4, S], BF16); B_KT = Buf()
        Vaug = sb("Vaug", [128, NB, 4, 129], BF16); B_Vaug = Buf()
        MSET(Vaug[:, :, :, 128:129], 1.0, [B_Vaug])
        QT = sb("QT", [128, 4, 128], BF16); B_QT = Buf()
        qkb = sb("qkb", [128, 2, 512], BF16); B_qkb = Buf()
        ET = [sb("ET%d" % k, [128, 4, 128], BF16) for k in range(2)]; B_ET = [Buf(), Buf()]
        dens = sb("dens", [128, 16], F32); B_dens = Buf()
        attb = sb("attb", [128, 512], BF16); B_attb = Buf()
        sa = sb("sa", [128, 8], F32); B_sa = Buf()
        attd = sb("attd", [128, 512], F32) if dbg else None; B_attd = Buf()

        def acc_slot(j):
            return PB[4 + j // 3][:, (j % 3) * 129:(j % 3) * 129 + 129], 4 + j // 3

        def subln_and_store_att(nt, tok0):
            for h in range(4):
                ACT(junk[0:nt, 0:128], attf[0:nt, h * 128:(h + 1) * 128], AF.Square, [B_attf], [B_junk, B_sa],
                    accum_out=sa[0:nt, h:h + 1])
            TS(sa[0:nt, 4:8], sa[0:nt, 0:4], 1.0 / 128, EPS, ALU.mult, ALU.add, [B_sa], [B_sa])
            ACT(sa[0:nt, 4:8], sa[0:nt, 4:8], AF.Sqrt, [B_sa], [B_sa])
            RECIP(sa[0:nt, 4:8], sa[0:nt, 4:8], [B_sa], [B_sa])
            for h in range(4):
                STT(attb[0:nt, h * 128:(h + 1) * 128], attf[0:nt, h * 128:(h + 1) * 128], sa[0:nt, 4 + h:5 + h],
                    gsubB[0:nt, :], ALU.mult, ALU.mult, [B_attf, B_sa, B_gsub], [B_attb])
                if dbg:
                    STT(attd[0:nt, h * 128:(h + 1) * 128], attf[0:nt, h * 128:(h + 1) * 128], sa[0:nt, 4 + h:5 + h],
                        gsubB[0:nt, :], ALU.mult, ALU.mult, [B_attf, B_sa, B_gsub], [B_attd])
            if dbg:
                DMA("sp", d_att[tok0:tok0 + nt, :], attd[0:nt, :], [B_attd], [])
            TR([(PT[:, h * 128:h * 128 + nt], attb[0:nt, h * 128:(h + 1) * 128], identb[0:nt, 0:nt])
                for h in range(4)], [B_attb, B_identb], [B_PT])
            CP(mixT[:, 0:4, tok0:tok0 + nt],
               PT[:, 0:512].rearrange("p (h t) -> p h t", h=4)[:, :, 0:nt], [B_PT], [B_mixT])

        NBUF = 2
        xt = [sb("xt%d" % i, [128, D], F32) for i in range(NBUF)]; B_xt = [Buf() for _ in range(NBUF)]
        junk = sb("junk", [128, D], BF16); B_junk = Buf()
        ss = [sb("ss%d" % i, [128, 2], F32) for i in range(NBUF)]; B_ss = [Buf() for _ in range(NBUF)]
        xn = [sb("xn%d" % i, [128, D], BF16) for i in range(NBUF)]; B_xn = [Buf() for _ in range(NBUF)]
        hT = [sb("hT%d" % i, [128, NC8, 128], BF16) for i in range(NBUF)]; B_hT = [Buf() for _ in range(NBUF)]
        kf0 = sb("kf0", [128, 512], F32); kf = [kf0] * NBUF; B_kf0 = Buf(); B_kf = [B_kf0] * NBUF
        vf0 = sb("vf0", [128, 512], F32); vf = [vf0] * NBUF; B_vf0 = Buf(); B_vf = [B_vf0] * NBUF
        Vr = sb("Vr", [128, 512], BF16); B_Vr = Buf()
        gG = sb("gG", [128, 512], F32); B_gG = Buf()
        qt = sb("qt", [128, 512], BF16); B_qt = Buf()
        kt = sb("kt", [128, 512], BF16); B_kt = Buf()
        kh = sb("kh", [128, 512], BF16); B_kh = Buf()
        kT = sb("kT", [128, 4, 128], BF16); B_kT = Buf()
        attm = sb("attm", [128, 4, 128], BF16); B_attm = Buf()
        ebl = sb("ebl", [128, 8], F32); B_ebl = Buf()
        rs = sb("rs", [128, 2], F32); B_rs = Buf()
        rec = sb("rec", [128, 512], BF16); B_rec = Buf()
        recf = sb("recf", [128, 512], F32) if dbg else None; B_recf = Buf()

        ones1 = sb("ones1", [128, 1], F32); B_ones1 = Buf()
        MSET(ones1[:], 1.0, [B_ones1])

        blocks = [("p", b, 128) for b in range(NB)] + [("s", 0, NS)]
        zi = [0]
        qs16 = sb("qs16", [16, 512], BF16); B_qs16 = Buf()

        def rstd_from(sumsq, dst, nt, dim, Bsrc, Bdst):
            TS(dst, sumsq, 1.0 / dim, EPS, ALU.mult, ALU.add, [Bsrc], [Bdst])
            ACT(dst, dst, AF.Sqrt, [Bdst], [Bdst])
            RECIP(dst, dst, [Bdst], [Bdst])

        def gate_and_store_rec(o_ps, B_o, nt, tok0, g_rows_B):
            ACT(junk[0:nt, 0:512], o_ps, AF.Square, [B_o], [B_junk, B_rs], accum_out=rs[0:nt, 0:1])
            rstd_from(rs[0:nt, 0:1], rs[0:nt, 1:2], nt, 512, B_rs, B_rs)
            STT(rec[0:nt, :], o_ps, rs[0:nt, 1:2], gG[0:nt, :], ALU.mult, ALU.mult,
                [B_o, B_rs, B_gG], [B_rec])
            if dbg:
                STT(recf[0:nt, :], o_ps, rs[0:nt, 1:2], gG[0:nt, :], ALU.mult, ALU.mult,
                    [B_o, B_rs, B_gG], [B_recf])
                DMA("sp", d_rec[tok0:tok0 + nt, :], recf[0:nt, :], [B_recf], [])
            TR([(PT[:, h * 128:h * 128 + nt], rec[0:nt, h * 128:(h + 1) * 128], identb[0:nt, 0:nt])
                for h in range(4)], [B_rec, B_identb], [B_PT])
            CP(mixT[:, 4:8, tok0:tok0 + nt],
               PT[:, 0:512].rearrange("p (h t) -> p h t", h=4)[:, :, 0:nt], [B_PT], [B_mixT])

        for bi, (kind, b, nt) in enumerate(blocks):
            i = bi % NBUF
            tok0 = b * 128 if kind == "p" else S
            src = x_p[b * 128:(b + 1) * 128, :] if kind == "p" else x_s[0:NS, :]
            DMA("sp", xt[i][0:nt, :], src, [], [B_xt[i]])
            ACT(junk[0:nt, :], xt[i][0:nt, :], AF.Square, [B_xt[i]], [B_junk, B_ss[i]], accum_out=ss[i][0:nt, 0:1])
            rstd_from(ss[i][0:nt, 0:1], ss[i][0:nt, 1:2], nt, D, B_ss[i], B_ss[i])
            ACT(xn[i][0:nt, :], xt[i][0:nt, :], AF.Copy, [B_xt[i], B_ss[i]], [B_xn[i]], scale=ss[i][0:nt, 1:2])
            TR([(PT[:, c * 128:c * 128 + nt], xn[i][0:nt, c * 128:(c + 1) * 128], identb[0:nt, 0:nt])
                for c in range(NC8)], [B_xn[i], B_identb], [B_PT])
            TT(hT[i][:, :, 0:nt], PT[:].rearrange("p (c t) -> p c t", c=NC8)[:, :, 0:nt],
               gmixT[:].unsqueeze(2).to_broadcast([128, NC8, nt]), ALU.mult, [B_PT, B_gmixT], [B_hT[i]])

            def zchunk(n):
                zb = zi[0] % 2
                zi[0] += 1
                MM([(PB[zb][0:nt, :], hT[i][:, c, 0:nt], w_sb[:, c, n * 512:(n + 1) * 512], c == 0, c == NC8 - 1)
                    for c in range(NC8)], [B_hT[i]] + B_w, [B_PB[zb]])
                return PB[zb][0:nt, :], B_PB[zb]

            for n in (1, 2):
                z, Bz = zchunk(n)
                dst, Bd = (kf, B_kf) if n == 1 else (vf, B_vf)
                CP(dst[i][0:nt, :], z, [Bz], [Bd[i]])
                if kind == "p":
                    o = (o_kp if n == 1 else o_vp)[b * 128:(b + 1) * 128, :]
                else:
                    o = (o_ks if n == 1 else o_vs)[0:NS, :]
                DMA("sp", o, dst[i][0:nt, :], [Bd[i]], [])

            if kind == "p":
                z, Bz = zchunk(0)
                CP(qkb[:, 0, :], z, [Bz], [B_qkb])
                CP(qkb[:, 1, :], kf[i][:, :], [B_kf[i]], [B_qkb])
                CP(Vaug[:, b, :, 0:128], vf[i][:, :].rearrange("p (h e) -> p h e", h=4), [B_vf[i]], [B_Vaug])
                TR([(PT[:, (j * 4 + h) * 128:(j * 4 + h + 1) * 128], qkb[:, j, h * 128:(h + 1) * 128], identb[:])
                    for j in range(2) for h in range(4)], [B_qkb, B_identb], [B_PT])
                PT8 = PT[:].rearrange("p (j t) -> p j t", j=8)
                CP(QT[:], PT8[:, 0:4, :], [B_PT], [B_QT])
                CP(KT[:, :, tok0:tok0 + 128], PT8[:, 4:8, :], [B_PT], [B_KT])
                for kb in range(b + 1):
                    for m in range(2):
                        MM([(PB[2 + m][:, h * 128:(h + 1) * 128],
                             KT[m * 64:(m + 1) * 64, h, kb * 128:(kb + 1) * 128],
                             QT[m * 64:(m + 1) * 64, h, :], True, True)
                            for h in range(4)], [B_KT, B_QT], [B_PB[2 + m]])
                        ACT(ET[m][:], PB[2 + m][:, :].rearrange("p (h q) -> p h q", h=4), AF.Exp,
                            [B_PB[2 + m]], [B_ET[m]], scale=64 ** -0.5)
                        if kb == b:
                            TT(ET[m][:], ET[m][:], TRI[:].unsqueeze(1).to_broadcast([128, 4, 128]), ALU.mult,
                               [B_ET[m], B_TRI], [B_ET[m]])
                    items = []
                    for j in range(8):
                        h, m = j // 2, j % 2
                        o_, bk = acc_slot(j)
                        last_in_bank = (j % 3 == 2) or (j == 7)
                        items.append((o_, ET[m][:, h, :], Vaug[:, kb, h, :],
                                      kb == 0 and j % 3 == 0, kb == b and last_in_bank))
                    MM(items, [B_ET[0], B_ET[1], B_Vaug], [B_PB[4], B_PB[5], B_PB[6]])
                for bk in range(3):
                    n_ = 3 if bk < 2 else 2
                    CP(dens[:, 3 * bk:3 * bk + n_],
                       PB[4 + bk][:, 0:n_ * 129].rearrange("p (j c) -> p j c", c=129)[:, :, 128], [B_PB[4 + bk]], [B_dens])
                RECIP(dens[:, 8:16], dens[:, 0:8], [B_dens], [B_dens])
                TS(dens[:, 8:16].rearrange("p (h m) -> p h m", m=2)[:, :, 1],
                   dens[:, 8:16].rearrange("p (h m) -> p h m", m=2)[:, :, 1], lamc[:, 3:4], None, ALU.mult, None,
                   [B_dens, B_lamc], [B_dens])
                for h in range(4):
                    o0, b0 = acc_slot(2 * h)
                    o1, b1 = acc_slot(2 * h + 1)
                    TS(attf[:, h * 128:(h + 1) * 128], o0[:, 0:128], dens[:, 8 + 2 * h:9 + 2 * h], None, ALU.mult, None,
                       [B_PB[b0], B_dens], [B_attf])
                    STT(attf[:, h * 128:(h + 1) * 128], o1[:, 0:128], dens[:, 9 + 2 * h:10 + 2 * h],
                        attf[:, h * 128:(h + 1) * 128], ALU.mult, ALU.add, [B_PB[b1], B_dens, B_attf], [B_attf])
                subln_and_store_att(128, tok0)

            if kind == "s":
                z, Bz = zchunk(0)
                CP(qs16[0:nt, :], z, [Bz], [B_qs16])

            z, Bz = zchunk(3)
            ACT(Qr[0:nt, :], z, AF.Silu, [Bz], [B_Qr])
            z, Bz = zchunk(4)
            ACT(Ff[0:nt, :], z, AF.Sigmoid, [Bz], [B_Ff])
            TT(Ff[0:nt, :], Ff[0:nt, :], omlB[0:nt, :], ALU.mult, [B_Ff, B_oml], [B_Ff])
            TT(Ff[0:nt, :], Ff[0:nt, :], lbB[0:nt, :], ALU.add, [B_Ff, B_lb], [B_Ff])
            ACT(LF[0:nt, :], Ff[0:nt, :], AF.Ln, [B_Ff], [B_LF])
            TS(KK[0:nt, :], Ff[0:nt, :], -1.0, 1.0, ALU.mult, ALU.add, [B_Ff], [B_KK])
            z, Bz = zchunk(5)
            CP(Vr[0:nt, :], z, [Bz], [B_Vr])
            z, Bz = zchunk(6)
            ACT(gG[0:nt, :], z, AF.Silu, [Bz], [B_gG])
            TT(gG[0:nt, :], gG[0:nt, :], grecB[0:nt, :], ALU.mult, [B_gG, B_grec], [B_gG])

            if kind == "p":
                MM([(PB[2][:, :], UU[:], LF[:], True, True)], [B_UU, B_LF], [B_PB[2]])
                MM([(PB[3][:, :], SS[:], LF[:], True, True)], [B_SS, B_LF], [B_PB[3]])
                MM([(PB[5][:, 2 * h:2 * h + 2], LF[:, h * 128:(h + 1) * 128], cm2[:], True, True)
                    for h in range(4)], [B_LF, B_cm2], [B_PB[5]])
                ACT(eb[:], PB[2][:, :], AF.Exp, [B_PB[2]], [B_eb])
                ACT(enb[:], PB[2][:, :], AF.Exp, [B_PB[2]], [B_enb], scale=-1.0)
                ACT(eD[:], PB[3][:, :], AF.Exp, [B_PB[3]], [B_eD])
                ACT(ebl[:], PB[5][:, 0:8], AF.Exp, [B_PB[5]], [B_ebl])
                TT(qt[:], Qr[:], eb[:], ALU.mult, [B_Qr, B_eb], [B_qt])
                TT(kt[:], KK[:], enb[:], ALU.mult, [B_KK, B_enb], [B_kt])
                TT(kh[:], KK[:], eD[:], ALU.mult, [B_KK, B_eD], [B_kh])
                TR([(PT[:, h * 128:(h + 1) * 128], qt[:, h * 128:(h + 1) * 128], identb[:]) for h in range(4)] +
                   [(PT[:, (4 + h) * 128:(5 + h) * 128], kt[:, h * 128:(h + 1) * 128], identb[:]) for h in range(4)],
                   [B_qt, B_kt, B_identb], [B_PT])
                PT4 = PT[:].rearrange("p (j t) -> p j t", j=8)
                CP(qT0[:, :, 0:64], PT4[:, 0:4, 0:64], [B_PT], [B_qT0])
                CP(qT1[:, :, 64:128], PT4[:, 0:4, 64:128], [B_PT], [B_qT1])
                CP(kT[:], PT4[:, 4:8, :], [B_PT], [B_kT])
                MM([(PB[4][:, h * 128:h * 128 + 64], kT[:, h, :], qT0[:, h, 0:64], True, True) for h in range(4)] +
                   [(PB[4][:, h * 128 + 64:(h + 1) * 128], kT[:, h, :], qT1[:, h, 64:128], True, True) for h in range(4)],
                   [B_kT, B_qT0, B_qT1], [B_PB[4]])
                TT(attm[:], PB[4][:, :].rearrange("p (h t) -> p h t", h=4),
                   UU[:].unsqueeze(1).to_broadcast([128, 4, 128]), ALU.mult, [B_PB[4], B_UU], [B_attm])
                for c in range(2):
                    MM([(PB[2 + c][:, h * 128:(h + 1) * 128], kh[c * 64:(c + 1) * 64, h * 128:(h + 1) * 128],
                         Vr[c * 64:(c + 1) * 64, h * 128:(h + 1) * 128], True, True) for h in range(4)],
                       [B_kh, B_Vr], [B_PB[2 + c]])
                for h in range(4):
                    STT(Sst[:, h, :], Sst[:, h, :], ebl[:, 2 * h:2 * h + 1], PB[2][:, h * 128:(h + 1) * 128],
                        ALU.mult, ALU.add, [B_S, B_ebl, B_PB[2]], [B_S])
                CP(Sb1[:], Sst[:], [B_S], [B_Sb1], eng="act")
                MM([t for h in range(4) for t in (
                    (PB[6][:, h * 128:(h + 1) * 128], attm[:, h, :], Vr[:, h * 128:(h + 1) * 128], True, False),
                    (PB[6][:, h * 128:(h + 1) * 128], qT0[:, h, :], Sb0[:, h, :], False, False),
                    (PB[6][:, h * 128:(h + 1) * 128], qT1[:, h, :], Sb1[:, h, :], False, True))],
                   [B_attm, B_Vr, B_qT0, B_qT1, B_Sb0, B_Sb1], [B_PB[6]])
                for h in range(4):
                    STT(Sst[:, h, :], Sst[:, h, :], ebl[:, 2 * h + 1:2 * h + 2], PB[3][:, h * 128:(h + 1) * 128],
                        ALU.mult, ALU.add, [B_S, B_ebl, B_PB[3]], [B_S])
                CP(Sb0[:], Sst[:], [B_S], [B_Sb0], eng="act")
                gate_and_store_rec(PB[6][:, :], B_PB[6], 128, tok0, B_gG)
                if b == NB - 1:
                    DMA("sp", o_sp.rearrange("(h d) v -> d h v", h=4), Sst[:], [B_S], [])
            else:
                fqk = sb("fqk", [128, 3, 4, 16], F32); B_fqk = Buf()
                TR([(PB[4][:, (j * 4 + h) * 16:(j * 4 + h) * 16 + nt], srcT[0:nt, h * 128:(h + 1) * 128], identf[0:nt, 0:nt])
                    for j, srcT in enumerate((Ff, KK, Qr)) for h in range(4)],
                   [B_Ff, B_KK, B_Qr, B_identf], [B_PB[4]])
                CP(fqk[:, :, :, 0:nt], PB[4][:, 0:192].rearrange("p (j h s) -> p j h s", j=3, h=4)[:, :, :, 0:nt],
                   [B_PB[4]], [B_fqk])
                oT = sb("oT", [128, 4, 16], F32); B_oT = Buf()
                v4 = lambda t: t.rearrange("p (h v) -> p h v", h=4)
                Ss = [v4(eb), v4(enb)]; B_Ss = [B_eb, B_enb]
                tmpv = v4(eD); B_tmpv = B_eD
                st_v = state_s.rearrange("(s h d) v -> s d h v", h=4, d=128)
                os_v = o_ss.rearrange("(s h d) v -> s d h v", h=4, d=128)
                for s_ in range(nt):
                    k2 = s_ % 2
                    DMA("sp", Ss[k2], st_v[s_], [], [B_Ss[k2]])
                    MM([(PB[5][:, :], E16[0:nt, s_, :], Vr[0:nt, :], True, True)], [B_E16, B_Vr], [B_PB[5]])
                    for h in range(4):
                        TS(tmpv[:, h, :], PB[5][:, h * 128:(h + 1) * 128], fqk[:, 1, h, s_:s_ + 1], None, ALU.mult, None,
                           [B_PB[5], B_fqk], [B_tmpv])
                        STT(Ss[k2][:, h, :], Ss[k2][:, h, :], fqk[:, 0, h, s_:s_ + 1], tmpv[:, h, :],
                            ALU.mult, ALU.add, [B_Ss[k2], B_fqk, B_tmpv], [B_Ss[k2]])
                    DMA("sp", os_v[s_], Ss[k2], [B_Ss[k2]], [])
                    MM([(PB[6][:, h * 16 + s_:h * 16 + s_ + 1], Ss[k2][:, h, :], fqk[:, 2, h, s_:s_ + 1], True, True)
                        for h in range(4)], [B_Ss[k2], B_fqk], [B_PB[6]])
                CP(oT[:, :, 0:nt], PB[6][:, 0:64].rearrange("p (h s) -> p h s", h=4)[:, :, 0:nt], [B_PB[6]], [B_oT])
                TR([(PB[3][0:nt, h * 128:(h + 1) * 128], oT[:, h, 0:nt], identf[:]) for h in range(4)],
                   [B_oT, B_identf], [B_PB[3]])
                gate_and_store_rec(PB[3][0:nt, :], B_PB[3], nt, tok0, B_gG)

        NJ = NPG + 1
        ptB = sb("ptB", [128, NS * NPG], I32); B_ptB = Buf()
        idx = sb("idx", [128, NS * NPG], I32); B_idx = Buf()
        DMA("sp", ptB[:], page_tab.unsqueeze(0).to_broadcast([128, NS * NPG]), [], [B_ptB])
        P.op("pool", lambda e: e.iota(idx[:], pattern=[[0, NS * NPG]], base=0, channel_multiplier=1), writes=[B_idx])
        TS(ptB[:], ptB[:], 7, None, ALU.logical_shift_left, None, [B_ptB], [B_ptB])
        TT(idx[:], idx[:], ptB[:], ALU.bitwise_or, [B_idx, B_ptB], [B_idx])
        coef = sb("coef", [128, 3], F32); B_coef = Buf()
        MSET(coef[:, 1:3], 1.0, [B_coef])
        P.op("pool", lambda e: e.affine_select(out=coef[:, 1:2], in_=coef[:, 1:2], pattern=[[0, 1]],
                                               compare_op=ALU.is_equal, fill=0.0, base=0, channel_multiplier=1),
             reads=[B_coef], writes=[B_coef])
        P.op("pool", lambda e: e.affine_select(out=coef[:, 2:3], in_=coef[:, 2:3], pattern=[[0, 1]],
                                               compare_op=ALU.is_equal, fill=0.0, base=-1, channel_multiplier=1),
             reads=[B_coef], writes=[B_coef])
        STT(coef[:, 0:1], coef[:, 2:3], lamc[:, 3:4], coef[:, 1:2], ALU.mult, ALU.add,
            [B_coef, B_lamc], [B_coef])
        SEL = sb("SEL", [2, 16, 16], F32); B_SEL = Buf()
        MSET(SEL[:], 1.0, [B_SEL])
        P.op("pool", lambda e: e.affine_select(out=SEL[:], in_=SEL[:], pattern=[[1, 16], [-1, 16]],
                                               compare_op=ALU.is_equal, fill=0.0, base=0, channel_multiplier=0),
             reads=[B_SEL], writes=[B_SEL])
        P.barrier()
        NKV = 4
        KVp = [wk[:, 2 * i_:2 * i_ + 2, :].rearrange("p t c -> p (t c)") for i_ in range(3)] + [xt[1][:]]
        B_KV = [Buf() for _ in range(NKV)]
        prod = wk[:, 6, :]; B_prod = Buf()
        qB = wk[:, 7, :]; B_qB = Buf()
        KVx = xt[0][:]; B_KVx = Buf()
        Vpb = [sb("Vpb%d" % k, [128, 4, 129], BF16) for k in range(2)]; B_Vpb = [Buf(), Buf()]
        for k in range(2):
            MSET(Vpb[k][:, :, 128:129], 1.0, [B_Vpb[k]])
        sc = sb("sc", [128, NJ, 8], F32); B_sc = [Buf() for _ in range(NJ)]
        Es = sb("Es", [128, NJ, 8], BF16); B_Es = [Buf() for _ in range(NJ)]
        MSET(Es[:, NPG, :], 0.0, [B_Es[NPG]])
        numN = sb("numN", [2, 512], F32); B_numN = Buf()
        rdc = sb("rdc", [2, 8], F32); B_rdc = Buf()
        scale = 64 ** -0.5
        gi = [0]
        for s_ in range(NS):
            MM([(PB[0][:, :], E16[0:NS, s_, :], qs16[0:NS, :], True, True)], [B_E16, B_qs16], [B_PB[0]])
            CP(qB, PB[0][:, :], [B_PB[0]], [B_qB], eng="act")
            DMA("sp", KVx[0:1, 0:512], kf0[s_:s_ + 1, :], [B_kf0], [B_KVx])
            DMA("sp", KVx[0:1, 512:1024], vf0[s_:s_ + 1, :], [B_vf0], [B_KVx])
            for j in range(NJ):
                k2 = (s_ * NJ + j) % 2
                if j < NPG:
                    kv = gi[0] % NKV
                    gi[0] += 1
                    col = s_ * NPG + j
                    P.dma("pool", (lambda kv, col: lambda e: e.indirect_dma_start(
                        out=KVp[kv], out_offset=None, in_=cache_kv,
                        in_offset=bass.IndirectOffsetOnAxis(ap=idx[:, col:col + 1], axis=0)))(kv, col),
                        reads=[B_idx], writes=[B_KV[kv]])
                    src, Bsrc, pr = KVp[kv], B_KV[kv], slice(0, 128)
                else:
                    src, Bsrc, pr = KVx, B_KVx, slice(0, 1)
                TT(prod[pr, :], src[pr, 0:512], qB[pr, :], ALU.mult, [Bsrc, B_qB], [B_prod])
                P.op("dve", (lambda j, pr: lambda e: e.tensor_reduce(
                    out=sc[pr, j, :], in_=prod[pr, :].rearrange("p (g d) -> p g d", d=64), axis=AX.X, op=ALU.add))(j, pr),
                    reads=[B_prod], writes=[B_sc[j]])
                ACT(Es[pr, j, :], sc[pr, j, :], AF.Exp, [B_sc[j]], [B_Es[j]], scale=scale)
                CP(Vpb[k2][:, :, 0:128], src[:, 512:1024].rearrange("p (h e) -> p h e", h=4), [Bsrc], [B_Vpb[k2]], eng="act")
                MM([(PB[1 + h // 3][0:2, (h % 3) * 129:(h % 3) * 129 + 129], Es[:, j, 2 * h:2 * h + 2], Vpb[k2][:, h, :],
                     j == 0 and h % 3 == 0, j == NJ - 1 and h in (2, 3)) for h in range(4)],
                   [B_Es[j], B_Vpb[k2]], [B_PB[1], B_PB[2]])
            for bk, hs in ((1, (0, 1, 2)), (2, (3,))):
                n_ = len(hs)
                CP(rdc[:, hs[0]:hs[0] + n_], PB[bk][0:2, 0:n_ * 129].rearrange("p (j c) -> p j c", c=129)[:, :, 128],
                   [B_PB[bk]], [B_rdc])
            RECIP(rdc[:, 4:8], rdc[:, 0:4], [B_rdc], [B_rdc])
            TS(rdc[:, 4:8], rdc[:, 4:8], coef[0:2, 0:1], None, ALU.mult, None, [B_rdc, B_coef], [B_rdc])
            for h in range(4):
                TS(numN[:, h * 128:(h + 1) * 128], PB[1 + h // 3][0:2, (h % 3) * 129:(h % 3) * 129 + 128],
                   rdc[:, 4 + h:5 + h], None, ALU.mult, None, [B_PB[1 + h // 3], B_rdc], [B_numN])
            MM([(PB[3][0:NS, :], SEL[:, s_, 0:NS], numN[:, :], s_ == 0, s_ == NS - 1)], [B_SEL, B_numN], [B_PB[3]])
        B_attf = B_KV[1]
        CP(attf[0:NS, :], PB[3][0:NS, :], [B_PB[3]], [B_attf])
        subln_and_store_att(NS, S)

        P.barrier()
        cur[0] = MARK
        NBLK = NB + 1
        blkinfo = [(x_p[b * 128:(b + 1) * 128, :], p_p[b * 128:(b + 1) * 128, :], o_yp[b * 128:(b + 1) * 128, :], 128, b * 128)
                   for b in range(NB)] + [(x_s[0:NS, :], p_s[0:NS, :], o_ys[0:NS, :], NS, S)]
        x1 = sb("x1", [128, NBLK, D], F32); B_x1 = [Buf() for _ in range(NBLK)]
        hT2 = mixT; B_hT2 = B_mixT
        wo_sb = sb("wo_sb", [128, NC8, D], BF16); B_wo = Buf()
        wg_sb = sb("wg_sb", [128, NC8, D], BF16); B_wg = Buf()
        wp_sb = sb("wp_sb", [128, 2, D], BF16); B_wp = Buf()
        FS = 512
        NSL = 4096 // FS
        wu = [sb("wu%d" % k, [128, NC8, FS], BF16) for k in range(2)]; B_wu = [Buf(), Buf()]
        wd = [sb("wd%d" % k, [128, FS // 128, D], BF16) for k in range(2)]; B_wd = [Buf(), Buf()]
        actT = [sb("actT%d" % k, [128, FS // 128, 512], BF16) for k in range(2)]; B_actT = [Buf(), Buf()]
        cxt = [sb("cxt%d" % k, [128, D], F32) for k in range(2)]; B_cxt = [Buf(), Buf()]
        cjunk = sb("cjunk", [128, D], BF16); B_cjunk = Buf()
        cxn = sb("cxn", [128, D], BF16); B_cxn = Buf()
        css = sb("css", [128, 2], F32); B_css = Buf()
        gffnT = sb("gffnT", [128, NC8], F32); B_gffnT = Buf()
        gfinB = sb("gfinB", [128, D], F32); B_gfin = Buf()
        x2T = sb("x2T", [128, NC8, 128], BF16); B_x2T = Buf()
        sg = sb("sg", [128, D], F32); B_sg = Buf()
        pf = sb("pf", [128, 256], F32); B_pf = Buf()
        pb16 = sb("pb16", [128, 256], BF16); B_pb16 = Buf()
        pT = sb("pT", [128, 2, 128], BF16); B_pT = Buf()
        yt = sb("yt", [128, D], F32); B_yt = Buf()

        DMA("pool", wo_sb[:], w_out.rearrange("(c p) n -> p c n", p=128), [], [B_wo])
        DMA("sp", gffnT[:], g_ffn.rearrange("(c p) -> p c", p=128), [], [B_gffnT], allow_slow_non_contiguous=True)
        DMA("sp", gfinB[:], g_final.unsqueeze(0).to_broadcast([128, D]), [], [B_gfin])

        def load_slice(sl):
            k = sl % 2
            DMA("pool", wu[k][:], w_up[:, sl * FS:(sl + 1) * FS].rearrange("(c p) f -> p c f", p=128), [], [B_wu[k]])
            DMA("pool", wd[k][:], w_down[sl * FS:(sl + 1) * FS, :].rearrange("(fc p) n -> p fc n", p=128), [], [B_wd[k]])

        for bi, (xsrc, psrc, ydst, nt, tok0) in enumerate(blkinfo):
            k = bi % 2
            DMA("sp", cxt[k][0:nt, :], xsrc, [], [B_cxt[k]])
            for nh in range(2):
                MM([(PB[nh][0:nt, :], mixT[:, c, tok0:tok0 + nt], wo_sb[:, c, nh * 512:(nh + 1) * 512], c == 0, c == NC8 - 1)
                    for c in range(NC8)], [B_mixT, B_wo], [B_PB[nh]])
                TT(x1[0:nt, bi, nh * 512:(nh + 1) * 512], cxt[k][0:nt, nh * 512:(nh + 1) * 512], PB[nh][0:nt, :], ALU.add,
                   [B_cxt[k], B_PB[nh]], [B_x1[bi]])
            ACT(cjunk[0:nt, :], x1[0:nt, bi, :], AF.Square, [B_x1[bi]], [B_cjunk, B_css], accum_out=css[0:nt, 0:1])
            rstd_from(css[0:nt, 0:1], css[0:nt, 1:2], nt, D, B_css, B_css)
            ACT(cxn[0:nt, :], x1[0:nt, bi, :], AF.Copy, [B_x1[bi], B_css], [B_cxn], scale=css[0:nt, 1:2])
            TR([(PT[:, c * 128:c * 128 + nt], cxn[0:nt, c * 128:(c + 1) * 128], identb[0:nt, 0:nt])
                for c in range(NC8)], [B_cxn, B_identb], [B_PT])
            TT(hT2[:, :, tok0:tok0 + nt], PT[:].rearrange("p (c t) -> p c t", c=NC8)[:, :, 0:nt],
               gffnT[:].unsqueeze(2).to_broadcast([128, NC8, nt]), ALU.mult, [B_PT, B_gffnT], [B_hT2])
            if bi == 0:
                load_slice(0)

        groups = [(g * 512, 512, list(range(g * 4, g * 4 + 4))) for g in range(S // 512)]
        if S % 512:
            g0 = (S // 512) * 512
            groups.append((g0, S - g0, list(range(g0 // 128, NB))))
        groups.append((S, NS, [NB]))
        units = [(sl, g) for sl in range(NSL) for g in range(len(groups))]

        def emit_up(i):
            sl, g = units[i]
            k, a = sl % 2, i % 2
            t0, ntk, blks = groups[g]
            for fc in range(FS // 128):
                MM([(PB[fc][:, 0:ntk], wu[k][:, c, fc * 128:(fc + 1) * 128], hT2[:, c, t0:t0 + ntk], c == 0, c == NC8 - 1)
                    for c in range(NC8)], [B_wu[k], B_hT2], [B_PB[fc]])
                ACT(actT[a][:, fc, 0:ntk], PB[fc][:, 0:ntk], AF.Relu, [B_PB[fc]], [B_actT[a]])
            TT(actT[a][:, :, 0:ntk], actT[a][:, :, 0:ntk], actT[a][:, :, 0:ntk], ALU.mult, [B_actT[a]], [B_actT[a]],
               eng="pool")

        def emit_down(i):
            sl, g = units[i]
            k, a = sl % 2, i % 2
            t0, ntk, blks = groups[g]
            for bj, blk in enumerate(blks):
                nt = blkinfo[blk][3]
                for nh in range(2):
                    MM([(PB[4 + nh][0:nt, :], actT[a][:, fc, bj * 128:bj * 128 + nt], wd[k][:, fc, nh * 512:(nh + 1) * 512],
                         fc == 0, fc == FS // 128 - 1) for fc in range(FS // 128)], [B_actT[a], B_wd[k]], [B_PB[4 + nh]])
                    TT(x1[0:nt, blk, nh * 512:(nh + 1) * 512], x1[0:nt, blk, nh * 512:(nh + 1) * 512], PB[4 + nh][0:nt, :],
                       ALU.add, [B_x1[blk], B_PB[4 + nh]], [B_x1[blk]])

        if NSL > 1:
            load_slice(1)
        emit_up(0)
        for i in range(len(units)):
            if i + 1 < len(units):
                emit_up(i + 1)
            emit_down(i)
            sl, g = units[i]
            if g == len(groups) - 1:
                if sl + 2 < NSL:
                    load_slice(sl + 2)
                if sl == 0:
                    DMA("pool", wg_sb[:], w_gate.rearrange("(c p) n -> p c n", p=128), [], [B_wg])
                    DMA("pool", wp_sb[:], w_proj.rearrange("(c p) n -> p c n", p=128), [], [B_wp])

        for bi, (xsrc, psrc, ydst, nt, tok0) in enumerate(blkinfo):
            DMA("sp", pf[0:nt, :], psrc, [], [B_pf])
            CP(pb16[0:nt, :], pf[0:nt, :], [B_pf], [B_pb16])
            ACT(cxn[0:nt, :], x1[0:nt, bi, :], AF.Copy, [B_x1[bi]], [B_cxn])
            TR([(PT[:, c * 128:c * 128 + nt], cxn[0:nt, c * 128:(c + 1) * 128], identb[0:nt, 0:nt])
                for c in range(NC8)], [B_cxn, B_identb], [B_PT])
            CP(x2T[:, :, 0:nt], PT[:].rearrange("p (c t) -> p c t", c=NC8)[:, :, 0:nt], [B_PT], [B_x2T])
            TR([(PT[:, c * 128:c * 128 + nt], pb16[0:nt, c * 128:(c + 1) * 128], identb[0:nt, 0:nt])
                for c in range(2)], [B_pb16, B_identb], [B_PT])
            CP(pT[:, :, 0:nt], PT[:, 0:256].rearrange("p (c t) -> p c t", c=2)[:, :, 0:nt], [B_PT], [B_pT])
            for nh in range(2):
                MM([(PB[nh][0:nt, :], x2T[:, c, 0:nt], wg_sb[:, c, nh * 512:(nh + 1) * 512], c == 0, c == NC8 - 1)
                    for c in range(NC8)], [B_x2T, B_wg], [B_PB[nh]])
                ACT(sg[0:nt, nh * 512:(nh + 1) * 512], PB[nh][0:nt, :], AF.Sigmoid, [B_PB[nh]], [B_sg])
                MM([(PB[2 + nh][0:nt, :], pT[:, c, 0:nt], wp_sb[:, c, nh * 512:(nh + 1) * 512], c == 0, c == 1)
                    for c in range(2)], [B_pT, B_wp], [B_PB[2 + nh]])
                TT(sg[0:nt, nh * 512:(nh + 1) * 512], sg[0:nt, nh * 512:(nh + 1) * 512], PB[2 + nh][0:nt, :], ALU.mult,
                   [B_sg, B_PB[2 + nh]], [B_sg])
            TT(x1[0:nt, bi, :], x1[0:nt, bi, :], sg[0:nt, :], ALU.add, [B_x1[bi], B_sg], [B_x1[bi]])
            ACT(cjunk[0:nt, :], x1[0:nt, bi, :], AF.Square, [B_x1[bi]], [B_cjunk, B_css], accum_out=css[0:nt, 0:1])
            rstd_from(css[0:nt, 0:1], css[0:nt, 1:2], nt, D, B_css, B_css)
            STT(yt[0:nt, :], x1[0:nt, bi, :], css[0:nt, 1:2], gfinB[0:nt, :], ALU.mult, ALU.mult,
                [B_x1[bi], B_css, B_gfin], [B_yt])
            DMA("sp", ydst, yt[0:nt, :], [B_yt], [])

        P.emit(st)
    return nc


def kernel(x_prompt, x_sample, cache_k, cache_v, state_hgrn, page_table, p_prompt, p_sample,
           w_in, lambda_q1, lambda_k1, lambda_q2, lambda_k2, g_subln, hgrn_lb, g_rec, w_out,
           g_mix, g_ffn, w_up, w_down, w_ple_gate, w_ple_proj, g_final):
    n = 8
    B, S, _ = x_prompt.shape
    NSA = x_sample.shape[0]
    NS = NSA // n
    f = lambda a: np.ascontiguousarray(np.asarray(a, dtype=np.float32))
    NPOOL = cache_k.shape[1]
    NPG = page_table.shape[1]
    nc = build_nc(S=S, NS=NS, NPG=NPG, NPOOL=NPOOL)
    ckv = np.concatenate([f(cache_k[0]).reshape(NPOOL * 128, 512), f(cache_v[0]).reshape(NPOOL * 128, 512)], axis=1)
    in_maps = []
    for c in range(n):
        in_maps.append({
            "x_p": f(x_prompt[c]),
            "x_s": f(x_sample[c * NS:(c + 1) * NS, 0]),
            "w_in": f(w_in[0]),
            "g_mix": f(g_mix[0]),
            "hgrn_lb": f(hgrn_lb),
            "lam_in": f(np.concatenate([lambda_q1, lambda_k1, lambda_q2, lambda_k2], 0)),
            "g_subln": f(g_subln[0]),
            "w_out": f(w_out[0]), "g_ffn": f(g_ffn[0]), "w_up": f(w_up[0]), "w_down": f(w_down[0]),
            "w_gate": f(w_ple_gate[0]), "w_proj": f(w_ple_proj[0]),
            "p_p": f(p_prompt[0, c]), "p_s": f(p_sample[0, c * NS:(c + 1) * NS, 0]),
            "g_final": f(g_final),
            "cache_kv": ckv,
            "page_tab": np.ascontiguousarray(np.asarray(page_table[c * NS:(c + 1) * NS], dtype=np.int32).reshape(-1)),
            "g_rec": f(g_rec[0]),
            "state_s": f(state_hgrn[0, c * NS:(c + 1) * NS]).reshape(NS * 4 * 128, 128),
        })
    res = run_bass_kernel_spmd(nc, in_maps, core_ids=list(range(n))).results
    cat = lambda k: np.stack([r[k] for r in res], 0)
    y_prompt = cat("o_yp").reshape(B, S, D)
    y_sample = cat("o_ys").reshape(NSA, 1, D)
    k_prompt = cat("o_kp").reshape(1, B, S, 4, 128)
    v_prompt = cat("o_vp").reshape(1, B, S, 4, 128)
    s_prompt = cat("o_sp").reshape(1, B, 4, 128, 128)
    k_sample = cat("o_ks").reshape(1, NSA, 1, 4, 128)
    v_sample = cat("o_vs").reshape(1, NSA, 1, 4, 128)
    s_sample = cat("o_ss").reshape(1, NSA, 4, 128, 128)
    return tuple(np.ascontiguousarray(a, dtype=np.float32) for a in
                 (y_prompt, y_sample, k_prompt, v_prompt, s_prompt, k_sample, v_sample, s_sample))
```

```python
from contextlib import ExitStack

import numpy as np
import concourse.bass as bass
import concourse.mybir as mybir
from concourse.bass_utils import run_bass_kernel_spmd

F32 = mybir.dt.float32
BF16 = mybir.dt.bfloat16
I32 = mybir.dt.int32
AF = mybir.ActivationFunctionType
ALU = mybir.AluOpType
AX = mybir.AxisListType

D = 1024
NC8 = 8
INW = 3584
EPS = 1e-6
COMPUTE = ("pe", "act", "dve", "pool")


class Ins:
    __slots__ = ("eng", "fn", "deps", "inc", "sem", "val", "is_dma")

    def __init__(self, eng, fn, is_dma=False):
        self.eng = eng
        self.fn = fn
        self.deps = []
        self.inc = False
        self.sem = None
        self.val = None
        self.is_dma = is_dma


class Buf:
    __slots__ = ("name", "w", "r")

    def __init__(self, name=""):
        self.name = name
        self.w = []
        self.r = []


class Prog:
    def __init__(self, nc, ndma=None):
        self.nc = nc
        self.q = {e: [] for e in ("pe", "act", "dve", "pool", "sp")}
        self.ndma = ndma or {"sp": 32, "pool": 24, "act": 8}
        self.dma_hist = {k: [] for k in self.ndma}
        self.all_dma = []
        self.pending = {}

    def barrier(self):
        deps = []
        for e, q in self.q.items():
            last = None
            for ins in reversed(q):
                if not ins.is_dma:
                    last = ins
                    break
            if last is not None:
                deps.append(last)
        for qn, hist in self.dma_hist.items():
            deps.extend(hist[-self.ndma[qn]:])
        for e in self.q:
            self.pending[e] = list(deps)

    def _track(self, ins, reads, writes, deps):
        ds = []
        for b in reads:
            ds.extend(b.w)
        for b in writes:
            ds.extend(b.w)
            ds.extend(b.r)
        ds.extend(deps)
        seen = set()
        for d in ds:
            if d is None or id(d) in seen or d is ins:
                continue
            seen.add(id(d))
            if (not d.is_dma) and (not ins.is_dma) and d.eng == "pe" and ins.eng == "pe":
                continue
            ins.deps.append(d)
            d.inc = True
        for b in writes:
            b.w = [ins]
            b.r = []
        for b in reads:
            if b not in writes:
                b.r.append(ins)

    def op(self, eng, fn, reads=(), writes=(), deps=()):
        ins = Ins(eng, fn)
        deps = list(deps) + self.pending.pop(eng, [])
        self._track(ins, reads, writes, deps)
        self.q[eng].append(ins)
        return ins

    def dma(self, queue, fn, reads=(), writes=(), deps=()):
        ins = Ins(queue, fn, is_dma=True)
        ins.inc = True
        hist = self.dma_hist[queue]
        n = self.ndma[queue]
        j = len(hist)
        extra = list(deps) + self.pending.pop(queue, [])
        if j >= n:
            extra.append(hist[j - n])
        self._track(ins, reads, writes, extra)
        ins.sem = (queue, j % n)
        ins.val = 16 * (j // n + 1)
        hist.append(ins)
        self.all_dma.append(ins)
        self.q[queue].append(ins)
        return ins

    def emit(self, stack):
        nc = self.nc
        esem = {e: stack.enter_context(nc.semaphore("s_" + e)) for e in COMPUTE}
        dsem = {}
        for qn, n in self.ndma.items():
            for i in range(n):
                dsem[(qn, i)] = stack.enter_context(nc.semaphore("d_%s%d" % (qn, i)))
        for e in ("pe", "act", "dve", "pool", "sp"):
            c = 0
            for ins in self.q[e]:
                if ins.is_dma:
                    ins.sem = dsem[ins.sem]
                elif ins.inc:
                    c += 1
                    ins.sem = esem[e]
                    ins.val = c
        final = {}
        for d in self.all_dma:
            k = d.sem.num
            if k not in final or final[k][1] < d.val:
                final[k] = (d.sem, d.val)
        block = stack.enter_context(nc.Block())

        def run(engname, eng):
            waited = {}
            for ins in self.q[engname]:
                for d in ins.deps:
                    k = d.sem.num
                    if waited.get(k, 0) >= d.val:
                        continue
                    eng.wait_ge(d.sem, d.val)
                    waited[k] = d.val
                bi = ins.fn(eng)
                if ins.inc:
                    bi.then_inc(ins.sem, 16 if ins.is_dma else 1)
            if engname == "sp":
                for k, (s, v) in final.items():
                    if waited.get(k, 0) < v:
                        eng.wait_ge(s, v)

        block.tensor(lambda e: run("pe", e))
        block.scalar(lambda e: run("act", e))
        block.vector(lambda e: run("dve", e))
        block.gpsimd(lambda e: run("pool", e))
        block.sync(lambda e: run("sp", e))


def build_nc(S=2048, NS=16, dbg=False, NPG=16, NPOOL=2560):
    assert S % 128 == 0
    NB = S // 128
    nc = bass.Bass("TRN2", target_bir_lowering=False)

    def din(name, shape, dt=F32):
        return nc.dram_tensor(name, shape, dt, kind="ExternalInput").ap()

    def dout(name, shape, dt=F32):
        return nc.dram_tensor(name, shape, dt, kind="ExternalOutput").ap()

    x_p = din("x_p", [S, D])
    x_s = din("x_s", [NS, D])
    w_in = din("w_in", [D, INW])
    g_mix = din("g_mix", [D])
    hgrn_lb = din("hgrn_lb", [2, 512])
    g_rec = din("g_rec", [512])
    state_s = din("state_s", [NS * 4 * 128, 128])
    lam_in = din("lam_in", [4, 64])
    g_subln = din("g_subln", [128])
    w_out = din("w_out", [D, D])
    g_ffn = din("g_ffn", [D])
    w_up = din("w_up", [D, 4096])
    w_down = din("w_down", [4096, D])
    w_gate = din("w_gate", [D, D])
    w_proj = din("w_proj", [256, D])
    p_p = din("p_p", [S, 256])
    p_s = din("p_s", [NS, 256])
    g_final = din("g_final", [D])
    cache_kv = din("cache_kv", [NPOOL * 128, 1024])
    page_tab = din("page_tab", [NS * NPG], I32)

    o_yp = dout("o_yp", [S, D])
    o_ys = dout("o_ys", [NS, D])
    o_kp = dout("o_kp", [S, 512])
    o_vp = dout("o_vp", [S, 512])
    o_sp = dout("o_sp", [4 * 128, 128])
    o_ks = dout("o_ks", [NS, 512])
    o_vs = dout("o_vs", [NS, 512])
    o_ss = dout("o_ss", [NS * 4 * 128, 128])
    if dbg:
        d_rec = dout("d_rec", [S + NS, 512])
        d_att = dout("d_att", [S + NS, 512])

    SB_LO, SB_HI = 16512, 229344
    cur = [SB_LO]
    DTB = {F32: 4, BF16: 2, I32: 4}

    with ExitStack() as st:
        def sb(name, shape, dt):
            n = DTB[dt]
            for d_ in shape[1:]:
                n *= d_
            off = (cur[0] + 31) // 32 * 32
            if off + n > SB_HI:
                raise AssertionError("SBUF overflow placing %s: need %d at %d (limit %d)" % (name, n, off, SB_HI))
            cur[0] = off + n
            return nc.alloc_sbuf_tensor_at(name, list(shape), dt, offset=off)

        P = Prog(nc)

        def ACT(out, in_, func, R, W, **kw):
            return P.op("act", lambda e: e.activation(out=out, in_=in_, func=func, **kw), reads=R, writes=W)

        def TT(out, in0, in1, op, R, W, eng="dve"):
            return P.op(eng, lambda e: e.tensor_tensor(out=out, in0=in0, in1=in1, op=op), reads=R, writes=W)

        def TS(out, in0, s1, s2, op0, op1, R, W, eng="dve"):
            if op1 is None:
                return P.op(eng, lambda e: e.tensor_scalar(out=out, in0=in0, scalar1=s1, scalar2=None, op0=op0),
                            reads=R, writes=W)
            return P.op(eng, lambda e: e.tensor_scalar(out=out, in0=in0, scalar1=s1, scalar2=s2, op0=op0, op1=op1),
                        reads=R, writes=W)

        def STT(out, in0, scalar, in1, op0, op1, R, W, eng="dve"):
            return P.op(eng, lambda e: e.scalar_tensor_tensor(out=out, in0=in0, scalar=scalar, in1=in1,
                                                              op0=op0, op1=op1), reads=R, writes=W)

        def CP(out, in_, R, W, eng="dve"):
            if eng == "act":
                return P.op("act", lambda e: e.activation(out=out, in_=in_, func=AF.Copy), reads=R, writes=W)
            return P.op(eng, lambda e: e.tensor_copy(out=out, in_=in_), reads=R, writes=W)

        def RECIP(out, in_, R, W):
            return P.op("dve", lambda e: e.reciprocal(out=out, in_=in_), reads=R, writes=W)

        def MSET(ap, v, W, eng="pool"):
            return P.op(eng, lambda e: e.memset(ap, v), writes=W)

        def ASEL(ap, pattern, cmp, cm, W):
            return P.op("pool", lambda e: e.affine_select(out=ap, in_=ap, pattern=pattern, compare_op=cmp,
                                                          fill=0.0, base=0, channel_multiplier=cm),
                        reads=W, writes=W)

        def DMA(q, out, in_, R, W, **kw):
            return P.dma(q, lambda e: e.dma_start(out=out, in_=in_, **kw), reads=R, writes=W)

        def MM(items, R, W):
            def f(e):
                r = None
                for (o, l, rr, s0, s1) in items:
                    r = e.matmul(o, lhsT=l, rhs=rr, start=s0, stop=s1)
                return r
            return P.op("pe", f, reads=R, writes=W)

        def TR(items, R, W):
            def f(e):
                r = None
                for (o, i_, idn) in items:
                    r = e.transpose(out=o, in_=i_, identity=idn)
                return r
            return P.op("pe", f, reads=R, writes=W)

        NTOK = S + NS
        identf = sb("identf", [128, 128], F32); B_identf = Buf()
        identb = sb("identb", [128, 128], BF16); B_identb = Buf()
        mixT = sb("mixT", [128, NC8, NTOK], BF16); B_mixT = Buf()
        MARK = cur[0]

        wk = sb("wk", [128, 8, 512], F32)
        eb, enb, eD, attf = wk[:, 0, :], wk[:, 1, :], wk[:, 2, :], wk[:, 3, :]
        Qr, Ff, LF, KK = wk[:, 4, :], wk[:, 5, :], wk[:, 6, :], wk[:, 7, :]
        B_eb, B_enb, B_eD, B_attf = Buf(), Buf(), Buf(), Buf()
        B_Qr, B_Ff, B_LF, B_KK = Buf(), Buf(), Buf(), Buf()

        gmixT = sb("gmixT", [128, NC8], F32); B_gmixT = Buf()
        UU = sb("UU", [128, 128], F32); B_UU = Buf()
        SS = sb("SS", [128, 128], F32); B_SS = Buf()
        cm2 = sb("cm2", [128, 2], F32); B_cm2 = Buf()
        E16 = sb("E16", [16, 16, 128], BF16); B_E16 = Buf()
        MSET(identf[:], 1.0, [B_identf])
        ASEL(identf[:], [[-1, 128]], ALU.is_equal, 1, [B_identf])
        CP(identb[:], identf[:], [B_identf], [B_identb])
        MSET(UU[:], 1.0, [B_UU])
        ASEL(UU[:], [[1, 128]], ALU.is_ge, -1, [B_UU])
        MSET(UU[0:64, 64:128], 0.0, [B_UU])
        MSET(SS[:], 1.0, [B_SS])
        ASEL(SS[:], [[-1, 128]], ALU.is_gt, 1, [B_SS])
        MSET(SS[64:128, 0:64], 0.0, [B_SS])
        MSET(cm2[:], 0.0, [B_cm2])
        MSET(cm2[0:64, 0:1], 1.0, [B_cm2])
        MSET(cm2[64:128, 1:2], 1.0, [B_cm2])
        MSET(E16[:], 1.0, [B_E16])
        ASEL(E16[:], [[-1, 16], [0, 128]], ALU.is_equal, 1, [B_E16])
        DMA("sp", gmixT[:], g_mix.rearrange("(c p) -> p c", p=128), [], [B_gmixT], allow_slow_non_contiguous=True)

        TRI = sb("TRI", [128, 128], BF16); B_TRI = Buf()
        trif = attf[:, 0:128]; B_trif = B_attf
        MSET(trif, 1.0, [B_trif])
        ASEL(trif, [[1, 128]], ALU.is_ge, -1, [B_trif])
        CP(TRI[:], trif, [B_trif], [B_TRI])
        LAM_INIT = 0.8 - 0.6 * float(np.exp(-0.3 * 0))
        lamB = sb("lamB", [128, 4, 64], F32); B_lamB = Buf()
        lamc = sb("lamc", [128, 4], F32); B_lamc = Buf()
        DMA("sp", lamB[:], lam_in.unsqueeze(0).to_broadcast([128, 4, 64]), [], [B_lamB])
        TT(lamB[:, 0, :], lamB[:, 0, :], lamB[:, 1, :], ALU.mult, [B_lamB], [B_lamB])
        TT(lamB[:, 2, :], lamB[:, 2, :], lamB[:, 3, :], ALU.mult, [B_lamB], [B_lamB])
        P.op("dve", lambda e: e.tensor_reduce(out=lamc[:, 0:1], in_=lamB[:, 0, :], axis=AX.X, op=ALU.add),
             reads=[B_lamB], writes=[B_lamc])
        P.op("dve", lambda e: e.tensor_reduce(out=lamc[:, 1:2], in_=lamB[:, 2, :], axis=AX.X, op=ALU.add),
             reads=[B_lamB], writes=[B_lamc])
        ACT(lamc[:, 0:2], lamc[:, 0:2], AF.Exp, [B_lamc], [B_lamc])
        TT(lamc[:, 2:3], lamc[:, 0:1], lamc[:, 1:2], ALU.subtract, [B_lamc], [B_lamc])
        TS(lamc[:, 2:3], lamc[:, 2:3], LAM_INIT, None, ALU.add, None, [B_lamc], [B_lamc])
        TS(lamc[:, 3:4], lamc[:, 2:3], -1.0, None, ALU.mult, None, [B_lamc], [B_lamc])
        gsubB = sb("gsubB", [128, 128], F32); B_gsub = Buf()
        DMA("sp", gsubB[:], g_subln.unsqueeze(0).to_broadcast([128, 128]), [], [B_gsub])
        TS(gsubB[:], gsubB[:], 1.0 - LAM_INIT, None, ALU.mult, None, [B_gsub], [B_gsub])

        l0, l1 = eb, enb
        lbB = sb("lbB", [128, 512], F32); omlB = sb("omlB", [128, 512], F32)
        grecB = sb("grecB", [128, 512], F32)
        B_l0, B_l1, B_lb, B_oml, B_grec = B_eb, B_enb, Buf(), Buf(), Buf()
        DMA("sp", l0[:], hgrn_lb[0:1, :].to_broadcast([128, 512]), [], [B_l0])
        DMA("sp", l1[:], hgrn_lb[1:2, :].to_broadcast([128, 512]), [], [B_l1])
        DMA("sp", grecB[:], g_rec.unsqueeze(0).to_broadcast([128, 512]), [], [B_grec])
        TT(lbB[:], l0[:], l1[:], ALU.max, [B_l0, B_l1], [B_lb])
        TT(l0[:], l0[:], lbB[:], ALU.subtract, [B_l0, B_lb], [B_l0])
        TT(l1[:], l1[:], lbB[:], ALU.subtract, [B_l1, B_lb], [B_l1])
        ACT(l0[:], l0[:], AF.Exp, [B_l0], [B_l0])
        ACT(l1[:], l1[:], AF.Exp, [B_l1], [B_l1])
        TT(l1[:], l0[:], l1[:], ALU.add, [B_l0, B_l1], [B_l1])
        RECIP(l1[:], l1[:], [B_l1], [B_l1])
        TT(lbB[:], l0[:], l1[:], ALU.mult, [B_l0, B_l1], [B_lb])
        TS(omlB[:], lbB[:], -1.0, 1.0, ALU.mult, ALU.add, [B_lb], [B_oml])

        w_sb = sb("w_sb", [128, NC8, INW], BF16)
        B_w = [Buf() for _ in range(NC8)]
        w_v = w_in.rearrange("(c p) n -> p c n", p=128)
        for c in range(NC8):
            DMA("pool", w_sb[:, c, :], w_v[:, c, :], [], [B_w[c]])

        PB = [st.enter_context(nc.psum_tensor("pb%d" % i, [128, 512], F32)) for i in range(7)]
        PT = st.enter_context(nc.psum_tensor("pt", [128, 1024], BF16))
        B_PB = [Buf() for _ in range(7)]
        B_PT = Buf()

        Sst = sb("Sst", [128, 4, 128], F32); B_S = Buf()
        Sb0 = sb("Sb0", [128, 4, 128], BF16); B_Sb0 = Buf()
        Sb1 = sb("Sb1", [128, 4, 128], BF16); B_Sb1 = Buf()
        qT0 = sb("qT0", [128, 4, 128], BF16); B_qT0 = Buf()
        qT1 = sb("qT1", [128, 4, 128], BF16); B_qT1 = Buf()
        MSET(Sst[:], 0.0, [B_S])
        MSET(Sb0[:], 0.0, [B_Sb0])
        MSET(qT0[:], 0.0, [B_qT0])
        MSET(qT1[:], 0.0, [B_qT1])

        KT = sb("KT", [128, 4, S], BF16); B_KT = Buf()
        Vaug = sb("Vaug", [128, NB, 4, 129], BF16); B_Vaug = Buf()
        MSET(Vaug[:, :, :, 128:129], 1.0, [B_Vaug])
        QT = sb("QT", [128, 4, 128], BF16); B_QT = Buf()
        qkb = sb("qkb", [128, 2, 512], BF16); B_qkb = Buf()
        ET = [sb("ET%d" % k, [128, 4, 128], BF16) for k in range(2)]; B_ET = [Buf(), Buf()]
        dens = sb("dens", [128, 16], F32); B_dens = Buf()
        attb = sb("attb", [128, 512], BF16); B_attb = Buf()
        sa = sb("sa", [128, 8], F32); B_sa = Buf()
        attd = sb("attd", [128, 512], F32) if dbg else None; B_attd = Buf()

        def acc_slot(j):
            return PB[4 + j // 3][:, (j % 3) * 129:(j % 3) * 129 + 129], 4 + j // 3

        def subln_and_store_att(nt, tok0):
            for h in range(4):
                ACT(junk[0:nt, 0:128], attf[0:nt, h * 128:(h + 1) * 128], AF.Square, [B_attf], [B_junk, B_sa],
                    accum_out=sa[0:nt, h:h + 1])
            TS(sa[0:nt, 4:8], sa[0:nt, 0:4], 1.0 / 128, EPS, ALU.mult, ALU.add, [B_sa], [B_sa])
            ACT(sa[0:nt, 4:8], sa[0:nt, 4:8], AF.Ln, [B_sa], [B_sa])
            ACT(sa[0:nt, 4:8], sa[0:nt, 4:8], AF.Exp, [B_sa], [B_sa], scale=-0.5)
            for h in range(4):
                STT(attb[0:nt, h * 128:(h + 1) * 128], attf[0:nt, h * 128:(h + 1) * 128], sa[0:nt, 4 + h:5 + h],
                    gsubB[0:nt, :], ALU.mult, ALU.mult, [B_attf, B_sa, B_gsub], [B_attb])
                if dbg:
                    STT(attd[0:nt, h * 128:(h + 1) * 128], attf[0:nt, h * 128:(h + 1) * 128], sa[0:nt, 4 + h:5 + h],
                        gsubB[0:nt, :], ALU.mult, ALU.mult, [B_attf, B_sa, B_gsub], [B_attd])
            if dbg:
                DMA("sp", d_att[tok0:tok0 + nt, :], attd[0:nt, :], [B_attd], [])
            TR([(PT[:, h * 128:h * 128 + nt], attb[0:nt, h * 128:(h + 1) * 128], identb[0:nt, 0:nt])
                for h in range(4)], [B_attb, B_identb], [B_PT])
            CP(mixT[:, 0:4, tok0:tok0 + nt],
               PT[:, 0:512].rearrange("p (h t) -> p h t", h=4)[:, :, 0:nt], [B_PT], [B_mixT])

        NBUF = 2
        xt = [sb("xt%d" % i, [128, D], F32) for i in range(NBUF)]; B_xt = [Buf() for _ in range(NBUF)]
        junk = sb("junk", [128, D], BF16); B_junk = Buf()
        ss = [sb("ss%d" % i, [128, 2], F32) for i in range(NBUF)]; B_ss = [Buf() for _ in range(NBUF)]
        xn = [sb("xn%d" % i, [128, D], BF16) for i in range(NBUF)]; B_xn = [Buf() for _ in range(NBUF)]
        hT = [sb("hT%d" % i, [128, NC8, 128], BF16) for i in range(NBUF)]; B_hT = [Buf() for _ in range(NBUF)]
        kf0 = sb("kf0", [128, 512], F32); kf = [kf0] * NBUF; B_kf0 = Buf(); B_kf = [B_kf0] * NBUF
        vf0 = sb("vf0", [128, 512], F32); vf = [vf0] * NBUF; B_vf0 = Buf(); B_vf = [B_vf0] * NBUF
        Vr = sb("Vr", [128, 512], BF16); B_Vr = Buf()
        gG = sb("gG", [128, 512], F32); B_gG = Buf()
        qt = sb("qt", [128, 512], BF16); B_qt = Buf()
        kt = sb("kt", [128, 512], BF16); B_kt = Buf()
        kh = sb("kh", [128, 512], BF16); B_kh = Buf()
        kT = sb("kT", [128, 4, 128], BF16); B_kT = Buf()
        attm = sb("attm", [128, 4, 128], BF16); B_attm = Buf()
        ebl = sb("ebl", [128, 8], F32); B_ebl = Buf()
        rs = sb("rs", [128, 2], F32); B_rs = Buf()
        rec = sb("rec", [128, 512], BF16); B_rec = Buf()
        recf = sb("recf", [128, 512], F32) if dbg else None; B_recf = Buf()

        ones1 = sb("ones1", [128, 1], F32); B_ones1 = Buf()
        MSET(ones1[:], 1.0, [B_ones1])

        blocks = [("p", b, 128) for b in range(NB)] + [("s", 0, NS)]
        zi = [0]
        qs16 = sb("qs16", [16, 512], BF16); B_qs16 = Buf()

        def rstd_from(sumsq, dst, nt, dim, Bsrc, Bdst):
            TS(dst, sumsq, 1.0 / dim, EPS, ALU.mult, ALU.add, [Bsrc], [Bdst])
            ACT(dst, dst, AF.Ln, [Bdst], [Bdst])
            ACT(dst, dst, AF.Exp, [Bdst], [Bdst], scale=-0.5)

        def gate_and_store_rec(o_ps, B_o, nt, tok0, g_rows_B):
            ACT(junk[0:nt, 0:512], o_ps, AF.Square, [B_o], [B_junk, B_rs], accum_out=rs[0:nt, 0:1])
            rstd_from(rs[0:nt, 0:1], rs[0:nt, 1:2], nt, 512, B_rs, B_rs)
            STT(rec[0:nt, :], o_ps, rs[0:nt, 1:2], gG[0:nt, :], ALU.mult, ALU.mult,
                [B_o, B_rs, B_gG], [B_rec])
            if dbg:
                STT(recf[0:nt, :], o_ps, rs[0:nt, 1:2], gG[0:nt, :], ALU.mult, ALU.mult,
                    [B_o, B_rs, B_gG], [B_recf])
                DMA("sp", d_rec[tok0:tok0 + nt, :], recf[0:nt, :], [B_recf], [])
            TR([(PT[:, h * 128:h * 128 + nt], rec[0:nt, h * 128:(h + 1) * 128], identb[0:nt, 0:nt])
                for h in range(4)], [B_rec, B_identb], [B_PT])
            CP(mixT[:, 4:8, tok0:tok0 + nt],
               PT[:, 0:512].rearrange("p (h t) -> p h t", h=4)[:, :, 0:nt], [B_PT], [B_mixT])

        def front_a(bj):
            kind_, b_, nt_ = blocks[bj]
            i_ = bj % NBUF
            src_ = x_p[b_ * 128:(b_ + 1) * 128, :] if kind_ == "p" else x_s[0:NS, :]
            DMA("sp", xt[i_][0:nt_, :], src_, [], [B_xt[i_]])
            ACT(junk[0:nt_, :], xt[i_][0:nt_, :], AF.Square, [B_xt[i_]], [B_junk, B_ss[i_]], accum_out=ss[i_][0:nt_, 0:1])
            rstd_from(ss[i_][0:nt_, 0:1], ss[i_][0:nt_, 1:2], nt_, D, B_ss[i_], B_ss[i_])
            ACT(xn[i_][0:nt_, :], xt[i_][0:nt_, :], AF.Copy, [B_xt[i_], B_ss[i_]], [B_xn[i_]], scale=ss[i_][0:nt_, 1:2])

        def front_b(bj):
            kind_, b_, nt_ = blocks[bj]
            i_ = bj % NBUF
            TR([(PT[:, c * 128:c * 128 + nt_], xn[i_][0:nt_, c * 128:(c + 1) * 128], identb[0:nt_, 0:nt_])
                for c in range(NC8)], [B_xn[i_], B_identb], [B_PT])
            TT(hT[i_][:, :, 0:nt_], PT[:].rearrange("p (c t) -> p c t", c=NC8)[:, :, 0:nt_],
               gmixT[:].unsqueeze(2).to_broadcast([128, NC8, nt_]), ALU.mult, [B_PT, B_gmixT], [B_hT[i_]])

        def next_is_prompt(bj):
            return bj + 1 < len(blocks) and blocks[bj + 1][0] == "p"

        front_a(0)
        front_b(0)
        for bi, (kind, b, nt) in enumerate(blocks):
            i = bi % NBUF
            tok0 = b * 128 if kind == "p" else S
            if kind == "s":
                front_a(bi)
                front_b(bi)

            def zchunk(n):
                zb = zi[0] % 2
                zi[0] += 1
                MM([(PB[zb][0:nt, :], hT[i][:, c, 0:nt], w_sb[:, c, n * 512:(n + 1) * 512], c == 0, c == NC8 - 1)
                    for c in range(NC8)], [B_hT[i]] + B_w, [B_PB[zb]])
                return PB[zb][0:nt, :], B_PB[zb]

            for n in (1, 2):
                z, Bz = zchunk(n)
                dst, Bd = (kf, B_kf) if n == 1 else (vf, B_vf)
                CP(dst[i][0:nt, :], z, [Bz], [Bd[i]])
                if kind == "p":
                    o = (o_kp if n == 1 else o_vp)[b * 128:(b + 1) * 128, :]
                else:
                    o = (o_ks if n == 1 else o_vs)[0:NS, :]
                DMA("sp", o, dst[i][0:nt, :], [Bd[i]], [])

            if next_is_prompt(bi):
                front_a(bi + 1)
            if kind == "p":
                z, Bz = zchunk(0)
                CP(qkb[:, 0, :], z, [Bz], [B_qkb])
                CP(qkb[:, 1, :], kf[i][:, :], [B_kf[i]], [B_qkb])
                CP(Vaug[:, b, :, 0:128], vf[i][:, :].rearrange("p (h e) -> p h e", h=4), [B_vf[i]], [B_Vaug])
                TR([(PT[:, (j * 4 + h) * 128:(j * 4 + h + 1) * 128], qkb[:, j, h * 128:(h + 1) * 128], identb[:])
                    for j in range(2) for h in range(4)], [B_qkb, B_identb], [B_PT])
                PT8 = PT[:].rearrange("p (j t) -> p j t", j=8)
                CP(QT[:], PT8[:, 0:4, :], [B_PT], [B_QT])
                CP(KT[:, :, tok0:tok0 + 128], PT8[:, 4:8, :], [B_PT], [B_KT])
                for kb in range(b + 1):
                    for m in range(2):
                        MM([(PB[2 + m][:, h * 128:(h + 1) * 128],
                             KT[m * 64:(m + 1) * 64, h, kb * 128:(kb + 1) * 128],
                             QT[m * 64:(m + 1) * 64, h, :], True, True)
                            for h in range(4)], [B_KT, B_QT], [B_PB[2 + m]])
                        ACT(ET[m][:], PB[2 + m][:, :].rearrange("p (h q) -> p h q", h=4), AF.Exp,
                            [B_PB[2 + m]], [B_ET[m]], scale=64 ** -0.5)
                        if kb == b:
                            TT(ET[m][:], ET[m][:], TRI[:].unsqueeze(1).to_broadcast([128, 4, 128]), ALU.mult,
                               [B_ET[m], B_TRI], [B_ET[m]])
                    items = []
                    for j in range(8):
                        h, m = j // 2, j % 2
                        o_, bk = acc_slot(j)
                        last_in_bank = (j % 3 == 2) or (j == 7)
                        items.append((o_, ET[m][:, h, :], Vaug[:, kb, h, :],
                                      kb == 0 and j % 3 == 0, kb == b and last_in_bank))
                    MM(items, [B_ET[0], B_ET[1], B_Vaug], [B_PB[4], B_PB[5], B_PB[6]])
                for bk in range(3):
                    n_ = 3 if bk < 2 else 2
                    CP(dens[:, 3 * bk:3 * bk + n_],
                       PB[4 + bk][:, 0:n_ * 129].rearrange("p (j c) -> p j c", c=129)[:, :, 128], [B_PB[4 + bk]], [B_dens])
                RECIP(dens[:, 8:16], dens[:, 0:8], [B_dens], [B_dens])
                TS(dens[:, 8:16].rearrange("p (h m) -> p h m", m=2)[:, :, 1],
                   dens[:, 8:16].rearrange("p (h m) -> p h m", m=2)[:, :, 1], lamc[:, 3:4], None, ALU.mult, None,
                   [B_dens, B_lamc], [B_dens])
                for h in range(4):
                    o0, b0 = acc_slot(2 * h)
                    o1, b1 = acc_slot(2 * h + 1)
                    TS(attf[:, h * 128:(h + 1) * 128], o0[:, 0:128], dens[:, 8 + 2 * h:9 + 2 * h], None, ALU.mult, None,
                       [B_PB[b0], B_dens], [B_attf])
                    STT(attf[:, h * 128:(h + 1) * 128], o1[:, 0:128], dens[:, 9 + 2 * h:10 + 2 * h],
                        attf[:, h * 128:(h + 1) * 128], ALU.mult, ALU.add, [B_PB[b1], B_dens, B_attf], [B_attf])
                subln_and_store_att(128, tok0)
            if next_is_prompt(bi):
                front_b(bi + 1)

            if kind == "s":
                z, Bz = zchunk(0)
                CP(qs16[0:nt, :], z, [Bz], [B_qs16])

            z, Bz = zchunk(3)
            ACT(Qr[0:nt, :], z, AF.Silu, [Bz], [B_Qr])
            z, Bz = zchunk(6)
            ACT(gG[0:nt, :], z, AF.Silu, [Bz], [B_gG])
            TT(gG[0:nt, :], gG[0:nt, :], grecB[0:nt, :], ALU.mult, [B_gG, B_grec], [B_gG])
            z, Bz = zchunk(4)
            ACT(Ff[0:nt, :], z, AF.Sigmoid, [Bz], [B_Ff])
            TT(Ff[0:nt, :], Ff[0:nt, :], omlB[0:nt, :], ALU.mult, [B_Ff, B_oml], [B_Ff])
            TT(Ff[0:nt, :], Ff[0:nt, :], lbB[0:nt, :], ALU.add, [B_Ff, B_lb], [B_Ff])
            ACT(LF[0:nt, :], Ff[0:nt, :], AF.Ln, [B_Ff], [B_LF])
            TS(KK[0:nt, :], Ff[0:nt, :], -1.0, 1.0, ALU.mult, ALU.add, [B_Ff], [B_KK])
            z, Bz = zchunk(5)
            CP(Vr[0:nt, :], z, [Bz], [B_Vr])

            if kind == "p":
                MM([(PB[2][:, :], UU[:], LF[:], True, True)], [B_UU, B_LF], [B_PB[2]])
                MM([(PB[3][:, :], SS[:], LF[:], True, True)], [B_SS, B_LF], [B_PB[3]])
                MM([(PB[5][:, 2 * h:2 * h + 2], LF[:, h * 128:(h + 1) * 128], cm2[:], True, True)
                    for h in range(4)], [B_LF, B_cm2], [B_PB[5]])
                ACT(eb[:], PB[2][:, :], AF.Exp, [B_PB[2]], [B_eb])
                ACT(enb[:], PB[2][:, :], AF.Exp, [B_PB[2]], [B_enb], scale=-1.0)
                ACT(eD[:], PB[3][:, :], AF.Exp, [B_PB[3]], [B_eD])
                ACT(ebl[:], PB[5][:, 0:8], AF.Exp, [B_PB[5]], [B_ebl])
                TT(qt[:], Qr[:], eb[:], ALU.mult, [B_Qr, B_eb], [B_qt])
                TT(kt[:], KK[:], enb[:], ALU.mult, [B_KK, B_enb], [B_kt])
                TT(kh[:], KK[:], eD[:], ALU.mult, [B_KK, B_eD], [B_kh])
                TR([(PT[:, h * 128:(h + 1) * 128], qt[:, h * 128:(h + 1) * 128], identb[:]) for h in range(4)] +
                   [(PT[:, (4 + h) * 128:(5 + h) * 128], kt[:, h * 128:(h + 1) * 128], identb[:]) for h in range(4)],
                   [B_qt, B_kt, B_identb], [B_PT])
                PT4 = PT[:].rearrange("p (j t) -> p j t", j=8)
                CP(qT0[:, :, 0:64], PT4[:, 0:4, 0:64], [B_PT], [B_qT0])
                CP(qT1[:, :, 64:128], PT4[:, 0:4, 64:128], [B_PT], [B_qT1])
                CP(kT[:], PT4[:, 4:8, :], [B_PT], [B_kT])
                MM([(PB[4][:, h * 128:h * 128 + 64], kT[:, h, :], qT0[:, h, 0:64], True, True) for h in range(4)] +
                   [(PB[4][:, h * 128 + 64:(h + 1) * 128], kT[:, h, :], qT1[:, h, 64:128], True, True) for h in range(4)],
                   [B_kT, B_qT0, B_qT1], [B_PB[4]])
                TT(attm[:], PB[4][:, :].rearrange("p (h t) -> p h t", h=4),
                   UU[:].unsqueeze(1).to_broadcast([128, 4, 128]), ALU.mult, [B_PB[4], B_UU], [B_attm])
                for c in range(2):
                    MM([(PB[2 + c][:, h * 128:(h + 1) * 128], kh[c * 64:(c + 1) * 64, h * 128:(h + 1) * 128],
                         Vr[c * 64:(c + 1) * 64, h * 128:(h + 1) * 128], True, True) for h in range(4)],
                       [B_kh, B_Vr], [B_PB[2 + c]])
                for h in range(4):
                    STT(Sst[:, h, :], Sst[:, h, :], ebl[:, 2 * h:2 * h + 1], PB[2][:, h * 128:(h + 1) * 128],
                        ALU.mult, ALU.add, [B_S, B_ebl, B_PB[2]], [B_S])
                CP(Sb1[:], Sst[:], [B_S], [B_Sb1], eng="act")
                MM([t for h in range(4) for t in (
                    (PB[6][:, h * 128:(h + 1) * 128], attm[:, h, :], Vr[:, h * 128:(h + 1) * 128], True, False),
                    (PB[6][:, h * 128:(h + 1) * 128], qT0[:, h, :], Sb0[:, h, :], False, False),
                    (PB[6][:, h * 128:(h + 1) * 128], qT1[:, h, :], Sb1[:, h, :], False, True))],
                   [B_attm, B_Vr, B_qT0, B_qT1, B_Sb0, B_Sb1], [B_PB[6]])
                for h in range(4):
                    STT(Sst[:, h, :], Sst[:, h, :], ebl[:, 2 * h + 1:2 * h + 2], PB[3][:, h * 128:(h + 1) * 128],
                        ALU.mult, ALU.add, [B_S, B_ebl, B_PB[3]], [B_S])
                CP(Sb0[:], Sst[:], [B_S], [B_Sb0], eng="act")
                gate_and_store_rec(PB[6][:, :], B_PB[6], 128, tok0, B_gG)
                if b == NB - 1:
                    DMA("sp", o_sp.rearrange("(h d) v -> d h v", h=4), Sst[:], [B_S], [])
            else:
                fqk = sb("fqk", [128, 3, 4, 16], F32); B_fqk = Buf()
                TR([(PB[4][:, (j * 4 + h) * 16:(j * 4 + h) * 16 + nt], srcT[0:nt, h * 128:(h + 1) * 128], identf[0:nt, 0:nt])
                    for j, srcT in enumerate((Ff, KK, Qr)) for h in range(4)],
                   [B_Ff, B_KK, B_Qr, B_identf], [B_PB[4]])
                CP(fqk[:, :, :, 0:nt], PB[4][:, 0:192].rearrange("p (j h s) -> p j h s", j=3, h=4)[:, :, :, 0:nt],
                   [B_PB[4]], [B_fqk])
                oT = sb("oT", [128, 4, 16], F32); B_oT = Buf()
                v4 = lambda t: t.rearrange("p (h v) -> p h v", h=4)
                Ss = [v4(eb), v4(enb)]; B_Ss = [B_eb, B_enb]
                tmpv = v4(eD); B_tmpv = B_eD
                st_v = state_s.rearrange("(s h d) v -> s d h v", h=4, d=128)
                os_v = o_ss.rearrange("(s h d) v -> s d h v", h=4, d=128)
                for s_ in range(nt):
                    k2 = s_ % 2
                    DMA("sp", Ss[k2], st_v[s_], [], [B_Ss[k2]])
                    MM([(PB[5][:, :], E16[0:nt, s_, :], Vr[0:nt, :], True, True)], [B_E16, B_Vr], [B_PB[5]])
                    for h in range(4):
                        TS(tmpv[:, h, :], PB[5][:, h * 128:(h + 1) * 128], fqk[:, 1, h, s_:s_ + 1], None, ALU.mult, None,
                           [B_PB[5], B_fqk], [B_tmpv])
                        STT(Ss[k2][:, h, :], Ss[k2][:, h, :], fqk[:, 0, h, s_:s_ + 1], tmpv[:, h, :],
                            ALU.mult, ALU.add, [B_Ss[k2], B_fqk, B_tmpv], [B_Ss[k2]])
                    DMA("sp", os_v[s_], Ss[k2], [B_Ss[k2]], [])
                    MM([(PB[6][:, h * 16 + s_:h * 16 + s_ + 1], Ss[k2][:, h, :], fqk[:, 2, h, s_:s_ + 1], True, True)
                        for h in range(4)], [B_Ss[k2], B_fqk], [B_PB[6]])
                CP(oT[:, :, 0:nt], PB[6][:, 0:64].rearrange("p (h s) -> p h s", h=4)[:, :, 0:nt], [B_PB[6]], [B_oT])
                TR([(PB[3][0:nt, h * 128:(h + 1) * 128], oT[:, h, 0:nt], identf[:]) for h in range(4)],
                   [B_oT, B_identf], [B_PB[3]])
                gate_and_store_rec(PB[3][0:nt, :], B_PB[3], nt, tok0, B_gG)

        NJ = NPG + 1
        ptB = sb("ptB", [128, NS * NPG], I32); B_ptB = Buf()
        idx = sb("idx", [128, NS * NPG], I32); B_idx = Buf()
        DMA("sp", ptB[:], page_tab.unsqueeze(0).to_broadcast([128, NS * NPG]), [], [B_ptB])
        P.op("pool", lambda e: e.iota(idx[:], pattern=[[0, NS * NPG]], base=0, channel_multiplier=1), writes=[B_idx])
        TS(ptB[:], ptB[:], 7, None, ALU.logical_shift_left, None, [B_ptB], [B_ptB])
        TT(idx[:], idx[:], ptB[:], ALU.bitwise_or, [B_idx, B_ptB], [B_idx])
        coef = sb("coef", [128, 3], F32); B_coef = Buf()
        MSET(coef[:, 1:3], 1.0, [B_coef])
        P.op("pool", lambda e: e.affine_select(out=coef[:, 1:2], in_=coef[:, 1:2], pattern=[[0, 1]],
                                               compare_op=ALU.is_equal, fill=0.0, base=0, channel_multiplier=1),
             reads=[B_coef], writes=[B_coef])
        P.op("pool", lambda e: e.affine_select(out=coef[:, 2:3], in_=coef[:, 2:3], pattern=[[0, 1]],
                                               compare_op=ALU.is_equal, fill=0.0, base=-1, channel_multiplier=1),
             reads=[B_coef], writes=[B_coef])
        STT(coef[:, 0:1], coef[:, 2:3], lamc[:, 3:4], coef[:, 1:2], ALU.mult, ALU.add,
            [B_coef, B_lamc], [B_coef])
        SEL = sb("SEL", [2, 16, 16], F32); B_SEL = Buf()
        MSET(SEL[:], 1.0, [B_SEL])
        P.op("pool", lambda e: e.affine_select(out=SEL[:], in_=SEL[:], pattern=[[1, 16], [-1, 16]],
                                               compare_op=ALU.is_equal, fill=0.0, base=0, channel_multiplier=0),
             reads=[B_SEL], writes=[B_SEL])
        P.barrier()
        NKV = 4
        KVp = [wk[:, 2 * i_:2 * i_ + 2, :].rearrange("p t c -> p (t c)") for i_ in range(3)] + [xt[1][:]]
        B_KV = [Buf() for _ in range(NKV)]
        prod = wk[:, 6, :]; B_prod = Buf()
        qB = wk[:, 7, :]; B_qB = Buf()
        KVx = xt[0][:]; B_KVx = Buf()
        Vpb = [sb("Vpb%d" % k, [128, 4, 129], BF16) for k in range(2)]; B_Vpb = [Buf(), Buf()]
        for k in range(2):
            MSET(Vpb[k][:, :, 128:129], 1.0, [B_Vpb[k]])
        sc = sb("sc", [128, NJ, 8], F32); B_sc = [Buf() for _ in range(NJ)]
        Es = sb("Es", [128, NJ, 8], BF16); B_Es = [Buf() for _ in range(NJ)]
        MSET(Es[:, NPG, :], 0.0, [B_Es[NPG]])
        numN = sb("numN", [2, 512], F32); B_numN = Buf()
        rdc = sb("rdc", [2, 8], F32); B_rdc = Buf()
        scale = 64 ** -0.5
        gi = [0]
        for s_ in range(NS):
            MM([(PB[0][:, :], E16[0:NS, s_, :], qs16[0:NS, :], True, True)], [B_E16, B_qs16], [B_PB[0]])
            CP(qB, PB[0][:, :], [B_PB[0]], [B_qB], eng="act")
            DMA("sp", KVx[0:1, 0:512], kf0[s_:s_ + 1, :], [B_kf0], [B_KVx])
            DMA("sp", KVx[0:1, 512:1024], vf0[s_:s_ + 1, :], [B_vf0], [B_KVx])
            for j in range(NJ):
                k2 = (s_ * NJ + j) % 2
                if j < NPG:
                    kv = gi[0] % NKV
                    gi[0] += 1
                    col = s_ * NPG + j
                    P.dma("pool", (lambda kv, col: lambda e: e.indirect_dma_start(
                        out=KVp[kv], out_offset=None, in_=cache_kv,
                        in_offset=bass.IndirectOffsetOnAxis(ap=idx[:, col:col + 1], axis=0)))(kv, col),
                        reads=[B_idx], writes=[B_KV[kv]])
                    src, Bsrc, pr = KVp[kv], B_KV[kv], slice(0, 128)
                else:
                    src, Bsrc, pr = KVx, B_KVx, slice(0, 1)
                TT(prod[pr, :], src[pr, 0:512], qB[pr, :], ALU.mult, [Bsrc, B_qB], [B_prod])
                P.op("dve", (lambda j, pr: lambda e: e.tensor_reduce(
                    out=sc[pr, j, :], in_=prod[pr, :].rearrange("p (g d) -> p g d", d=64), axis=AX.X, op=ALU.add))(j, pr),
                    reads=[B_prod], writes=[B_sc[j]])
                ACT(Es[pr, j, :], sc[pr, j, :], AF.Exp, [B_sc[j]], [B_Es[j]], scale=scale)
                CP(Vpb[k2][:, :, 0:128], src[:, 512:1024].rearrange("p (h e) -> p h e", h=4), [Bsrc], [B_Vpb[k2]], eng="act")
                MM([(PB[1 + h // 3][0:2, (h % 3) * 129:(h % 3) * 129 + 129], Es[:, j, 2 * h:2 * h + 2], Vpb[k2][:, h, :],
                     j == 0 and h % 3 == 0, j == NJ - 1 and h in (2, 3)) for h in range(4)],
                   [B_Es[j], B_Vpb[k2]], [B_PB[1], B_PB[2]])
            for bk, hs in ((1, (0, 1, 2)), (2, (3,))):
                n_ = len(hs)
                CP(rdc[:, hs[0]:hs[0] + n_], PB[bk][0:2, 0:n_ * 129].rearrange("p (j c) -> p j c", c=129)[:, :, 128],
                   [B_PB[bk]], [B_rdc])
            RECIP(rdc[:, 4:8], rdc[:, 0:4], [B_rdc], [B_rdc])
            TS(rdc[:, 4:8], rdc[:, 4:8], coef[0:2, 0:1], None, ALU.mult, None, [B_rdc, B_coef], [B_rdc])
            for h in range(4):
                TS(numN[:, h * 128:(h + 1) * 128], PB[1 + h // 3][0:2, (h % 3) * 129:(h % 3) * 129 + 128],
                   rdc[:, 4 + h:5 + h], None, ALU.mult, None, [B_PB[1 + h // 3], B_rdc], [B_numN])
            MM([(PB[3][0:NS, :], SEL[:, s_, 0:NS], numN[:, :], s_ == 0, s_ == NS - 1)], [B_SEL, B_numN], [B_PB[3]])
        B_attf = B_KV[1]
        CP(attf[0:NS, :], PB[3][0:NS, :], [B_PB[3]], [B_attf])
        subln_and_store_att(NS, S)

        P.barrier()
        cur[0] = MARK
        NBLK = NB + 1
        blkinfo = [(x_p[b * 128:(b + 1) * 128, :], p_p[b * 128:(b + 1) * 128, :], o_yp[b * 128:(b + 1) * 128, :], 128, b * 128)
                   for b in range(NB)] + [(x_s[0:NS, :], p_s[0:NS, :], o_ys[0:NS, :], NS, S)]
        x1 = sb("x1", [128, NBLK, D], F32); B_x1 = [Buf() for _ in range(NBLK)]
        hT2 = mixT; B_hT2 = B_mixT
        wo_sb = sb("wo_sb", [128, NC8, D], BF16); B_wo = Buf()
        wg_sb = sb("wg_sb", [128, NC8, D], BF16); B_wg = Buf()
        wp_sb = sb("wp_sb", [128, 2, D], BF16); B_wp = Buf()
        FS = 512
        NSL = 4096 // FS
        wu = [sb("wu%d" % k, [128, NC8, FS], BF16) for k in range(2)]; B_wu = [Buf(), Buf()]
        wd = [sb("wd%d" % k, [128, FS // 128, D], BF16) for k in range(2)]; B_wd = [Buf(), Buf()]
        actT = [sb("actT%d" % k, [128, FS // 128, 512], BF16) for k in range(2)]; B_actT = [Buf(), Buf()]
        cxt = [sb("cxt%d" % k, [128, D], F32) for k in range(2)]; B_cxt = [Buf(), Buf()]
        cjunk = sb("cjunk", [128, D], BF16); B_cjunk = Buf()
        cxn = sb("cxn", [128, D], BF16); B_cxn = Buf()
        css = sb("css", [128, 2], F32); B_css = Buf()
        gffnT = sb("gffnT", [128, NC8], F32); B_gffnT = Buf()
        gfinB = sb("gfinB", [128, D], F32); B_gfin = Buf()
        x2T = sb("x2T", [128, NC8, 128], BF16); B_x2T = Buf()
        sg = sb("sg", [128, D], F32); B_sg = Buf()
        pf = sb("pf", [128, 256], F32); B_pf = Buf()
        pb16 = sb("pb16", [128, 256], BF16); B_pb16 = Buf()
        pT = sb("pT", [128, 2, 128], BF16); B_pT = Buf()
        yt = sb("yt", [128, D], F32); B_yt = Buf()

        DMA("pool", wo_sb[:], w_out.rearrange("(c p) n -> p c n", p=128), [], [B_wo])
        DMA("sp", gffnT[:], g_ffn.rearrange("(c p) -> p c", p=128), [], [B_gffnT], allow_slow_non_contiguous=True)
        DMA("sp", gfinB[:], g_final.unsqueeze(0).to_broadcast([128, D]), [], [B_gfin])

        def load_slice(sl):
            k = sl % 2
            DMA("pool", wu[k][:], w_up[:, sl * FS:(sl + 1) * FS].rearrange("(c p) f -> p c f", p=128), [], [B_wu[k]])
            DMA("pool", wd[k][:], w_down[sl * FS:(sl + 1) * FS, :].rearrange("(fc p) n -> p fc n", p=128), [], [B_wd[k]])

        def c1_mm(bi):
            xsrc, psrc, ydst, nt, tok0 = blkinfo[bi]
            k = bi % 2
            DMA("sp", cxt[k][0:nt, :], xsrc, [], [B_cxt[k]])
            for nh in range(2):
                MM([(PB[nh][0:nt, :], mixT[:, c, tok0:tok0 + nt], wo_sb[:, c, nh * 512:(nh + 1) * 512], c == 0, c == NC8 - 1)
                    for c in range(NC8)], [B_mixT, B_wo], [B_PB[nh]])
                TT(x1[0:nt, bi, nh * 512:(nh + 1) * 512], cxt[k][0:nt, nh * 512:(nh + 1) * 512], PB[nh][0:nt, :], ALU.add,
                   [B_cxt[k], B_PB[nh]], [B_x1[bi]])

        def c1_norm(bi):
            xsrc, psrc, ydst, nt, tok0 = blkinfo[bi]
            ACT(cjunk[0:nt, :], x1[0:nt, bi, :], AF.Square, [B_x1[bi]], [B_cjunk, B_css], accum_out=css[0:nt, 0:1])
            rstd_from(css[0:nt, 0:1], css[0:nt, 1:2], nt, D, B_css, B_css)
            ACT(cxn[0:nt, :], x1[0:nt, bi, :], AF.Copy, [B_x1[bi], B_css], [B_cxn], scale=css[0:nt, 1:2])
            TR([(PT[:, c * 128:c * 128 + nt], cxn[0:nt, c * 128:(c + 1) * 128], identb[0:nt, 0:nt])
                for c in range(NC8)], [B_cxn, B_identb], [B_PT])
            TT(hT2[:, :, tok0:tok0 + nt], PT[:].rearrange("p (c t) -> p c t", c=NC8)[:, :, 0:nt],
               gffnT[:].unsqueeze(2).to_broadcast([128, NC8, nt]), ALU.mult, [B_PT, B_gffnT], [B_hT2])

        c1_mm(0)
        load_slice(0)
        for bi in range(NB):
            if bi + 1 < NB:
                c1_mm(bi + 1)
            c1_norm(bi)
        c1_mm(NB)
        c1_norm(NB)

        groups = [(g * 512, 512, list(range(g * 4, g * 4 + 4))) for g in range(S // 512)]
        if S % 512:
            g0 = (S // 512) * 512
            groups.append((g0, S - g0, list(range(g0 // 128, NB))))
        groups.append((S, NS, [NB]))
        units = [(sl, g) for sl in range(NSL) for g in range(len(groups))]

        def emit_up(i):
            sl, g = units[i]
            k, a = sl % 2, i % 2
            t0, ntk, blks = groups[g]
            for fc in range(FS // 128):
                MM([(PB[fc][:, 0:ntk], wu[k][:, c, fc * 128:(fc + 1) * 128], hT2[:, c, t0:t0 + ntk], c == 0, c == NC8 - 1)
                    for c in range(NC8)], [B_wu[k], B_hT2], [B_PB[fc]])
                ACT(actT[a][:, fc, 0:ntk], PB[fc][:, 0:ntk], AF.Relu, [B_PB[fc]], [B_actT[a]])
            TT(actT[a][:, :, 0:ntk], actT[a][:, :, 0:ntk], actT[a][:, :, 0:ntk], ALU.mult, [B_actT[a]], [B_actT[a]],
               eng="pool")

        dn = [0]

        def emit_down(i):
            sl, g = units[i]
            k, a = sl % 2, i % 2
            t0, ntk, blks = groups[g]
            for bj, blk in enumerate(blks):
                nt = blkinfo[blk][3]
                for nh in range(2):
                    pb = 4 + dn[0] % 3
                    dn[0] += 1
                    MM([(PB[pb][0:nt, :], actT[a][:, fc, bj * 128:bj * 128 + nt], wd[k][:, fc, nh * 512:(nh + 1) * 512],
                         fc == 0, fc == FS // 128 - 1) for fc in range(FS // 128)], [B_actT[a], B_wd[k]], [B_PB[pb]])
                    TT(x1[0:nt, blk, nh * 512:(nh + 1) * 512], x1[0:nt, blk, nh * 512:(nh + 1) * 512], PB[pb][0:nt, :],
                       ALU.add, [B_x1[blk], B_PB[pb]], [B_x1[blk]])

        if NSL > 1:
            load_slice(1)
        emit_up(0)
        for i in range(len(units)):
            if i + 1 < len(units):
                emit_up(i + 1)
            emit_down(i)
            sl, g = units[i]
            if g == len(groups) - 1:
                if sl + 2 < NSL:
                    load_slice(sl + 2)
                if sl == 0:
                    DMA("pool", wg_sb[:], w_gate.rearrange("(c p) n -> p c n", p=128), [], [B_wg])
                    DMA("pool", wp_sb[:], w_proj.rearrange("(c p) n -> p c n", p=128), [], [B_wp])

        def c3_gate(bi):
            xsrc, psrc, ydst, nt, tok0 = blkinfo[bi]
            DMA("sp", pf[0:nt, :], psrc, [], [B_pf])
            CP(pb16[0:nt, :], pf[0:nt, :], [B_pf], [B_pb16])
            CP(cxn[0:nt, :], x1[0:nt, bi, :], [B_x1[bi]], [B_cxn])
            TR([(PT[:, c * 128:c * 128 + nt], cxn[0:nt, c * 128:(c + 1) * 128], identb[0:nt, 0:nt])
                for c in range(NC8)], [B_cxn, B_identb], [B_PT])
            CP(x2T[:, :, 0:nt], PT[:].rearrange("p (c t) -> p c t", c=NC8)[:, :, 0:nt], [B_PT], [B_x2T])
            TR([(PT[:, c * 128:c * 128 + nt], pb16[0:nt, c * 128:(c + 1) * 128], identb[0:nt, 0:nt])
                for c in range(2)], [B_pb16, B_identb], [B_PT])
            CP(pT[:, :, 0:nt], PT[:, 0:256].rearrange("p (c t) -> p c t", c=2)[:, :, 0:nt], [B_PT], [B_pT])
            for nh in range(2):
                MM([(PB[nh][0:nt, :], x2T[:, c, 0:nt], wg_sb[:, c, nh * 512:(nh + 1) * 512], c == 0, c == NC8 - 1)
                    for c in range(NC8)], [B_x2T, B_wg], [B_PB[nh]])
                ACT(sg[0:nt, nh * 512:(nh + 1) * 512], PB[nh][0:nt, :], AF.Sigmoid, [B_PB[nh]], [B_sg])
                MM([(PB[2 + nh][0:nt, :], pT[:, c, 0:nt], wp_sb[:, c, nh * 512:(nh + 1) * 512], c == 0, c == 1)
                    for c in range(2)], [B_pT, B_wp], [B_PB[2 + nh]])
                TT(sg[0:nt, nh * 512:(nh + 1) * 512], sg[0:nt, nh * 512:(nh + 1) * 512], PB[2 + nh][0:nt, :], ALU.mult,
                   [B_sg, B_PB[2 + nh]], [B_sg])
            TT(x1[0:nt, bi, :], x1[0:nt, bi, :], sg[0:nt, :], ALU.add, [B_x1[bi], B_sg], [B_x1[bi]])

        def c3_out(bi):
            xsrc, psrc, ydst, nt, tok0 = blkinfo[bi]
            ACT(cjunk[0:nt, :], x1[0:nt, bi, :], AF.Square, [B_x1[bi]], [B_cjunk, B_css], accum_out=css[0:nt, 0:1])
            rstd_from(css[0:nt, 0:1], css[0:nt, 1:2], nt, D, B_css, B_css)
            STT(yt[0:nt, :], x1[0:nt, bi, :], css[0:nt, 1:2], gfinB[0:nt, :], ALU.mult, ALU.mult,
                [B_x1[bi], B_css, B_gfin], [B_yt])
            DMA("sp", ydst, yt[0:nt, :], [B_yt], [])

        c3_gate(0)
        for bi in range(NB):
            if bi + 1 < NB:
                c3_gate(bi + 1)
            c3_out(bi)
        c3_gate(NB)
        c3_out(NB)

        P.emit(st)
    return nc


def kernel(x_prompt, x_sample, cache_k, cache_v, state_hgrn, page_table, p_prompt, p_sample,
           w_in, lambda_q1, lambda_k1, lambda_q2, lambda_k2, g_subln, hgrn_lb, g_rec, w_out,
           g_mix, g_ffn, w_up, w_down, w_ple_gate, w_ple_proj, g_final):
    n = 8
    B, S, _ = x_prompt.shape
    NSA = x_sample.shape[0]
    NS = NSA // n
    f = lambda a: np.ascontiguousarray(np.asarray(a, dtype=np.float32))
    NPOOL = cache_k.shape[1]
    NPG = page_table.shape[1]
    nc = build_nc(S=S, NS=NS, NPG=NPG, NPOOL=NPOOL)
    ckv = np.concatenate([f(cache_k[0]).reshape(NPOOL * 128, 512), f(cache_v[0]).reshape(NPOOL * 128, 512)], axis=1)
    in_maps = []
    for c in range(n):
        in_maps.append({
            "x_p": f(x_prompt[c]),
            "x_s": f(x_sample[c * NS:(c + 1) * NS, 0]),
            "w_in": f(w_in[0]),
            "g_mix": f(g_mix[0]),
            "hgrn_lb": f(hgrn_lb),
            "lam_in": f(np.concatenate([lambda_q1, lambda_k1, lambda_q2, lambda_k2], 0)),
            "g_subln": f(g_subln[0]),
            "w_out": f(w_out[0]), "g_ffn": f(g_ffn[0]), "w_up": f(w_up[0]), "w_down": f(w_down[0]),
            "w_gate": f(w_ple_gate[0]), "w_proj": f(w_ple_proj[0]),
            "p_p": f(p_prompt[0, c]), "p_s": f(p_sample[0, c * NS:(c + 1) * NS, 0]),
            "g_final": f(g_final),
            "cache_kv": ckv,
            "page_tab": np.ascontiguousarray(np.asarray(page_table[c * NS:(c + 1) * NS], dtype=np.int32).reshape(-1)),
            "g_rec": f(g_rec[0]),
            "state_s": f(state_hgrn[0, c * NS:(c + 1) * NS]).reshape(NS * 4 * 128, 128),
        })
    res = run_bass_kernel_spmd(nc, in_maps, core_ids=list(range(n))).results
    cat = lambda k: np.stack([r[k] for r in res], 0)
    y_prompt = cat("o_yp").reshape(B, S, D)
    y_sample = cat("o_ys").reshape(NSA, 1, D)
    k_prompt = cat("o_kp").reshape(1, B, S, 4, 128)
    v_prompt = cat("o_vp").reshape(1, B, S, 4, 128)
    s_prompt = cat("o_sp").reshape(1, B, 4, 128, 128)
    k_sample = cat("o_ks").reshape(1, NSA, 1, 4, 128)
    v_sample = cat("o_vs").reshape(1, NSA, 1, 4, 128)
    s_sample = cat("o_ss").reshape(1, NSA, 4, 128, 128)
    return tuple(np.ascontiguousarray(a, dtype=np.float32) for a in
                 (y_prompt, y_sample, k_prompt, v_prompt, s_prompt, k_sample, v_sample, s_sample))
```
